# Optimizing a Trainium2 kernel written in Bass

```python
import math
import jax, jax.numpy as jnp
from jax import lax
import numpy as np

D_MODEL = 2048
BATCH = 2
SEQ = 4096
DEPTH = 4

N_MIXERS = 2
N_POOL_LAYERS = (DEPTH + 1) // 2
N_FOX_LAYERS = DEPTH // 2

POOL_WINDOWS = (2, 4, 8, 16)
N_POOL_GROUPS = len(POOL_WINDOWS)
POOL_GROUP = D_MODEL // N_POOL_GROUPS

FOX_HEAD_DIM = 128
FOX_HEADS = D_MODEL // FOX_HEAD_DIM
Q_BLOCK = 128

D_FF = 5632
FFN_RESIDUAL_WEIGHT = 0.5
RMS_EPS = 1e-6
NEG_LARGE = -1e30

kernel_name = "hybrid_pool_fox_macaron"


def rms_norm(x, g):
    xf = x.astype(jnp.float32)
    y = xf * lax.rsqrt(jnp.mean(xf * xf, axis=-1, keepdims=True) + RMS_EPS)
    return (y * g.astype(jnp.float32)).astype(x.dtype)


def swiglu(x, w_gate, w_up, w_down):
    return (jax.nn.silu(x @ w_gate) * (x @ w_up)) @ w_down


def causal_trailing_mean(xg, window):
    T = xg.shape[1]
    c = jnp.pad(jnp.cumsum(xg.astype(jnp.float32), axis=1), ((0, 0), (1, 0), (0, 0)))
    hi = c[:, 1:]
    lo = jnp.pad(c[:, :T + 1 - window], ((0, 0), (window - 1, 0), (0, 0)))
    count = jnp.minimum(jnp.arange(1, T + 1), window).astype(jnp.float32)[None, :, None]
    return (hi - lo) / count


def pool_mixer(h, w_groups, scale):
    B, T, D = h.shape
    hg = h.reshape(B, T, N_POOL_GROUPS, POOL_GROUP)
    pooled = jnp.stack([causal_trailing_mean(hg[:, :, g], w) for g, w in enumerate(POOL_WINDOWS)], axis=2)
    diff = (pooled - hg.astype(jnp.float32)).astype(h.dtype)
    y = jnp.einsum('btgc,gcd->btgd', diff, w_groups).reshape(B, T, D)
    return y * scale


def fox_mixer(h, w_in, b_f, q_gain, k_gain, w_out):
    B, T, D = h.shape
    proj = h @ w_in
    q = proj[..., :D].reshape(B, T, FOX_HEADS, FOX_HEAD_DIM)
    k = proj[..., D:2 * D].reshape(B, T, FOX_HEADS, FOX_HEAD_DIM)
    v = proj[..., 2 * D:3 * D].reshape(B, T, FOX_HEADS, FOX_HEAD_DIM)
    f_logit = proj[..., 3 * D:]
    q = rms_norm(q, q_gain) * (1.0 / math.sqrt(FOX_HEAD_DIM))
    k = rms_norm(k, k_gain)
    log_f = jax.nn.log_sigmoid((f_logit + b_f).astype(jnp.float32))
    F = jnp.cumsum(log_f, axis=1).transpose(0, 2, 1)
    q = q.transpose(0, 2, 1, 3)
    k = k.transpose(0, 2, 1, 3)
    v = v.transpose(0, 2, 1, 3)
    nb = T // Q_BLOCK
    q_blocks = q.reshape(B, FOX_HEADS, nb, Q_BLOCK, FOX_HEAD_DIM).transpose(2, 0, 1, 3, 4)
    F_blocks = F.reshape(B, FOX_HEADS, nb, Q_BLOCK).transpose(2, 0, 1, 3)
    starts = jnp.arange(nb, dtype=jnp.int32) * Q_BLOCK
    kpos = jnp.arange(T, dtype=jnp.int32)

    def attend_block(args):
        qi, Fi, s0 = args
        logits = jnp.einsum('bhqd,bhkd->bhqk', qi, k).astype(jnp.float32)
        logits = logits + Fi[..., :, None] - F[..., None, :]
        qpos = s0 + jnp.arange(Q_BLOCK, dtype=jnp.int32)
        mask = kpos[None, :] <= qpos[:, None]
        logits = jnp.where(mask, logits, NEG_LARGE)
        p = jax.nn.softmax(logits, axis=-1).astype(v.dtype)
        return jnp.einsum('bhqk,bhkd->bhqd', p, v)

    o = lax.map(attend_block, (q_blocks, F_blocks, starts))
    o = o.transpose(1, 0, 3, 2, 4).reshape(B, T, D)
    return o @ w_out


def setup_inputs(seed: int = 0) -> dict:
    key = jax.random.key(seed)
    ks = iter(jax.random.split(key, 32))
    nrm = lambda shape, s: jax.random.normal(next(ks), shape, jnp.float32) * s
    gain = lambda shape: 1.0 + nrm(shape, 0.02)
    D, F, H = D_MODEL, D_FF, FOX_HEADS
    return {
        "x": nrm((BATCH, SEQ, D), 1.0),
        "ffn1_norm": gain((DEPTH, D)),
        "ffn1_w_gate": nrm((DEPTH, D, F), D ** -0.5),
        "ffn1_w_up": nrm((DEPTH, D, F), D ** -0.5),
        "ffn1_w_down": nrm((DEPTH, F, D), F ** -0.5),
        "mix_norm": gain((DEPTH, D)),
        "pool_w": nrm((N_POOL_LAYERS, N_POOL_GROUPS, POOL_GROUP, POOL_GROUP), POOL_GROUP ** -0.5),
        "pool_scale": 1.0 + nrm((N_POOL_LAYERS, D), 0.1),
        "fox_w_in": nrm((N_FOX_LAYERS, D, 3 * D + H), D ** -0.5),
        "fox_b_f": nrm((N_FOX_LAYERS, H), 0.1),
        "fox_q_gain": gain((N_FOX_LAYERS, FOX_HEAD_DIM)),
        "fox_k_gain": gain((N_FOX_LAYERS, FOX_HEAD_DIM)),
        "fox_w_out": nrm((N_FOX_LAYERS, D, D), D ** -0.5),
        "ffn2_norm": gain((DEPTH, D)),
        "ffn2_w_gate": nrm((DEPTH, D, F), D ** -0.5),
        "ffn2_w_up": nrm((DEPTH, D, F), D ** -0.5),
        "ffn2_w_down": nrm((DEPTH, F, D), F ** -0.5),
    }


def reference(x, ffn1_norm, ffn1_w_gate, ffn1_w_up, ffn1_w_down, mix_norm, pool_w, pool_scale,
              fox_w_in, fox_b_f, fox_q_gain, fox_k_gain, fox_w_out,
              ffn2_norm, ffn2_w_gate, ffn2_w_up, ffn2_w_down):
    for i in range(DEPTH):
        x = x + FFN_RESIDUAL_WEIGHT * swiglu(rms_norm(x, ffn1_norm[i]), ffn1_w_gate[i], ffn1_w_up[i], ffn1_w_down[i])
        h = rms_norm(x, mix_norm[i])
        j = i // N_MIXERS
        if i % N_MIXERS == 0:
            x = x + pool_mixer(h, pool_w[j], pool_scale[j])
        else:
            x = x + fox_mixer(h, fox_w_in[j], fox_b_f[j], fox_q_gain[j], fox_k_gain[j], fox_w_out[j])
        x = x + FFN_RESIDUAL_WEIGHT * swiglu(rms_norm(x, ffn2_norm[i]), ffn2_w_gate[i], ffn2_w_up[i], ffn2_w_down[i])
    return x
```

```python
import numpy as np
import ml_dtypes
import concourse.bass as bass
import concourse.mybir as mybir
from concourse.bass_utils import run_bass_kernel_spmd

F32 = mybir.dt.float32
BF16 = mybir.dt.bfloat16
AF = mybir.ActivationFunctionType
ALU = mybir.AluOpType

D = 2048
FF = 5632
T = 1024
SEQ = 4096
KC = D // 128
NFC = FF // 128
G = 4
NG = NFC // G
DEPTH = 4
EPS = 1e-6
NH = 4
DH = 128

V_F1 = 0
V_F2 = V_F1 + 4 * KC
V_MX = V_F2 + 4 * KC
V_PS = V_MX + 4 * KC
V_QG = V_PS + 2 * KC
V_KG = V_QG + 2
V_HF = V_KG + 2
NV = V_HF + 1


class Sem:
    def __init__(self, h):
        self.h = h
        self.n = 0


class Prog:
    def __init__(self):
        self.q = {e: [] for e in ("pe", "act", "dve", "pool", "sp")}
        self.waited = {e: {} for e in self.q}

    def op(self, eng, fn, deps=(), sem=None, by=1):
        ws = []
        for d in deps:
            if d is None:
                continue
            s, v = d
            if v <= 0:
                continue
            if self.waited[eng].get(id(s), 0) >= v:
                continue
            self.waited[eng][id(s)] = v
            ws.append((s.h, v))
        tok = None
        if sem is not None:
            sem.n += by
            tok = (sem, sem.n)

        def run(e, ws=ws, fn=fn, sem=sem, by=by):
            for h, v in ws:
                e.wait_ge(h, v)
            ins = fn(e)
            if sem is not None:
                ins.then_inc(sem.h, by)

        self.q[eng].append(run)
        return tok

    def wait_only(self, eng, deps):
        ws = []
        for d in deps:
            if d is None:
                continue
            s, v = d
            if self.waited[eng].get(id(s), 0) >= v:
                continue
            self.waited[eng][id(s)] = v
            ws.append((s.h, v))

        def run(e, ws=ws):
            for h, v in ws:
                e.wait_ge(h, v)

        self.q[eng].append(run)


class Builder:
    def __init__(self, stages, n_in_extra=None):
        self.stages = stages
        self.nc = bass.Bass("TRN2", target_bir_lowering=False)
        self.P = Prog()
        self.dram_in = {}
        self.dram_out = {}

    def din(self, name, shape, dt=F32):
        if name not in self.dram_in:
            self.dram_in[name] = self.nc.dram_tensor(name, list(shape), dt, kind="ExternalInput").ap()
        return self.dram_in[name]

    def dout(self, name, shape, dt=F32):
        if name not in self.dram_out:
            self.dram_out[name] = self.nc.dram_tensor(name, list(shape), dt, kind="ExternalOutput").ap()
        return self.dram_out[name]

    def build(self):
        nc = self.nc
        P = self.P
        from contextlib import ExitStack
        with ExitStack() as es:
            def sb(name, shape, dt):
                return es.enter_context(nc.sbuf_tensor("sb_" + name, list(shape), dt))

            def sem(name):
                return Sem(es.enter_context(nc.semaphore(name)))

            self.fused = any(st[0] == "fused" for st in self.stages)
            self.stages = [st for st in self.stages if st[0] != "fused"]
            self.xT = sb("xT", [128, KC, T], F32)
            self.hn = sb("hn", [128, KC, T], BF16)
            self.sgh = sb("sgh", [128, 2 * G, T], BF16)
            self.sg = self.sgh[:, 0:G, :]
            self.hT = self.sgh[:, G:2 * G, :]
            self.rstd = sb("rstd", [128, T], F32)
            self.ring = [sb(f"ring{i}", [128, 8192], BF16) for i in range(4)]
            self.hn_flat = self.hn[:].rearrange("p a b -> p (a b)")
            self.hn_f32 = self.hn_flat.bitcast(F32)
            self.sgh_flat = self.sgh[:].rearrange("p a b -> p (a b)")
            self.sgh_f32 = self.sgh_flat.bitcast(F32)
            self.ones = sb("ones_sb", [128, 128], BF16)
            self.vecs = sb("vecs_sb", [128, NV], F32)
            self.ps = [es.enter_context(nc.psum_tensor(f"ps{i}", [128, 512], F32)) for i in range(8)]
            self.es_sb = sb

            self.s_pe = sem("s_pe")
            self.s_act = sem("s_act")
            self.s_dve = sem("s_dve")
            self.s_pool = sem("s_pool")
            self.s_ring = [sem(f"s_ring{i}") for i in range(4)]
            self.s_ld = sem("s_ld")
            self.s_st = sem("s_st")
            self.s_ob = [sem("s_ob0"), sem("s_ob1")]
            self.s_cc = [sem("s_cc0"), sem("s_cc1")]
            self.s_cw = [sem("s_cw0"), sem("s_cw1")]
            self.s_cr = [sem("s_cr0"), sem("s_cr1")]
            self.all_sems = [self.s_pe, self.s_act, self.s_dve, self.s_pool, self.s_ld, self.s_st] + self.s_ring + self.s_ob + self.s_cc + self.s_cw + self.s_cr
            self.src_free = [None, None]
            self.dst_free = [None, None]
            if self.fused:
                self.cc_src = [nc.dram_tensor(f"cc_src{i}", [128, 2048], BF16) for i in range(2)]
                self.cc_dst = [nc.dram_tensor(f"cc_dst{i}", [512, 2048], BF16) for i in range(2)]
                self.hn_all_d = nc.dram_tensor("hn_all_i", [128, KC, SEQ], BF16).ap()
                self.oT_all_d = nc.dram_tensor("oT_all_i", [128, KC, SEQ], BF16).ap()

            self.ring_idx = 0
            self.ring_free = [None] * 4
            self.bank_free = [None] * 8
            self.gu_rr = 0
            self.dn_rr = 0
            self.x_ready = None
            self.hn_free = None
            self.sg_free = [None] * G
            self.hT_free = [None] * G
            self.rstd_free = None
            self.pre_sq = None

            t_ones = P.op("pool", lambda e: e.memset(self.ones[:], 1.0), sem=self.s_pool)
            self.ones_ready = t_ones
            vecs_d = self.din("vecs", [128, NV])
            self.vecs_ready = P.op("sp", lambda e: e.dma_start(out=self.vecs[:], in_=vecs_d), sem=self.s_ld, by=16)

            if self.fused:
                self.cc_go(0)
            for st in self.stages:
                getattr(self, "st_" + st[0])(*st[1:])

            P.wait_only("sp", [(x, x.n) for x in self.all_sems])

            blk = es.enter_context(nc.Block())
            q = P.q

            @blk.tensor
            def _(e):
                for f in q["pe"]:
                    f(e)

            @blk.scalar
            def _(e):
                for f in q["act"]:
                    f(e)

            @blk.vector
            def _(e):
                for f in q["dve"]:
                    f(e)

            @blk.gpsimd
            def _(e):
                for f in q["pool"]:
                    f(e)

            @blk.sync
            def _(e):
                if self.fused:
                    self.pid_sp = nc.partition_id(engines=[mybir.EngineType.SP])
                for f in q["sp"]:
                    f(e)
        return nc


    def barrier(self):
        deps = [(x, x.n) for x in self.all_sems]
        for eng in ("pe", "act", "dve", "pool", "sp"):
            self.P.wait_only(eng, deps)

    GROUPS = [[0, 1, 2, 3], [4, 5, 6, 7]]

    def cc_write(self, i, fn, deps):
        return self.P.op("sp", lambda e: fn(e, self.cc_src[i]), deps=list(deps) + [self.src_free[i]], sem=self.s_cw[i], by=16)

    def cc_go(self, i):
        P = self.P
        tok = P.op("pool", lambda e: e.collective_compute("AllGather", ALU.bypass, replica_groups=self.GROUPS,
                                                          ins=[self.cc_src[i].ap()], outs=[self.cc_dst[i].ap()]),
                   deps=[(self.s_cw[i], self.s_cw[i].n), self.dst_free[i]], sem=self.s_cc[i], by=1)
        self.src_free[i] = tok
        return tok

    def cc_read(self, i, fn, ctok, eng="pool"):
        tok = self.P.op(eng, lambda e: fn(e, self.cc_dst[i]), deps=[ctok], sem=self.s_cr[i], by=16)
        self.dst_free[i] = tok
        return tok
    def load_w(self, src_ap, view, eng="pool"):
        P = self.P
        r = self.ring_idx
        self.ring_idx += 1
        slot = r % 4
        dst = view(self.ring[slot])
        tok = P.op(eng, lambda e: e.dma_start(out=dst, in_=src_ap),
                   deps=[self.ring_free[slot]], sem=self.s_ring[slot], by=16)
        return slot, tok

    def gu_banks(self):
        i = self.gu_rr % 2
        self.gu_rr += 1
        return (0, 1) if i == 0 else (2, 3)

    def dn_banks(self):
        i = self.dn_rr % 2
        self.dn_rr += 1
        return (4, 5) if i == 0 else (6, 7)

    def st_load_x(self):
        P = self.P
        xin = self.din("xT_in", [128, KC, T])
        for i in range(4):
            P.op("sp", lambda e, i=i: e.dma_start(out=self.xT[:, 4 * i:4 * i + 4, :], in_=xin[:, 4 * i:4 * i + 4, :]),
                 sem=self.s_ld, by=16)
        self.x_ready = (self.s_ld, self.s_ld.n)

    def st_store_x(self):
        P = self.P
        xout = self.dout("xT_out", [128, KC, T])
        for i in range(4):
            P.op("sp", lambda e, i=i: e.dma_start(out=xout[:, 4 * i:4 * i + 4, :], in_=self.xT[:, 4 * i:4 * i + 4, :]),
                 deps=[self.x_ready], sem=self.s_st, by=16)

    def rmsnorm(self, vcol):
        P = self.P
        xT, hn, rstd = self.xT, self.hn, self.rstd
        if self.pre_sq:
            sq_tok = self.pre_sq
        else:
            sq_tok = []
            for kc in range(KC):
                sq_tok.append(P.op("act", lambda e, kc=kc: e.activation(out=hn[:, kc, :], in_=xT[:, kc, :], func=AF.Square),
                                   deps=[self.x_ready, self.hn_free], sem=self.s_act))
        self.pre_sq = None
        banks = self.gu_banks()
        for th in range(2):
            for kc in range(KC):
                last = kc == KC - 1
                tok = P.op("pe", lambda e, th=th, kc=kc: e.matmul(self.ps[banks[th]][:], self.ones[:], hn[:, kc, th * 512:(th + 1) * 512],
                                                                   start=(kc == 0), stop=(kc == KC - 1)),
                           deps=[sq_tok[kc], self.ones_ready, self.bank_free[banks[th]]],
                           sem=self.s_pe if last else None)
        ssq_tok = tok
        for th in range(2):
            sl = slice(th * 512, (th + 1) * 512)
            t1 = P.op("dve", lambda e, th=th, sl=sl: e.tensor_scalar(out=rstd[:, sl], in0=self.ps[banks[th]][:], scalar1=1.0 / D, scalar2=EPS,
                                                                      op0=ALU.mult, op1=ALU.add),
                      deps=[ssq_tok, self.rstd_free], sem=self.s_dve)
            self.bank_free[banks[th]] = t1
            t15 = P.op("act", lambda e, sl=sl: e.activation(out=rstd[:, sl], in_=rstd[:, sl], func=AF.Sqrt),
                       deps=[t1], sem=self.s_act)
            t2 = P.op("dve", lambda e, sl=sl: e.reciprocal(out=rstd[:, sl], in_=rstd[:, sl]),
                      deps=[t15], sem=self.s_dve)
        toks = []
        for kc in range(KC):
            tok = P.op("dve", lambda e, kc=kc: e.scalar_tensor_tensor(out=hn[:, kc, :], in0=xT[:, kc, :],
                                                                       scalar=self.vecs[:, vcol + kc:vcol + kc + 1], in1=rstd[:],
                                                                       op0=ALU.mult, op1=ALU.mult),
                       deps=[t2, ssq_tok, self.vecs_ready], sem=self.s_dve)
            toks.append(tok)
        self.hn_ready = tok
        self.rstd_free = tok
        return toks

    def st_ffn(self, wname, layer, vcol):
        P = self.P
        wg = self.din(wname + "g", [D, FF]).rearrange("(kc p) f -> p kc f", p=128)
        wu = self.din(wname + "u", [D, FF]).rearrange("(kc p) f -> p kc f", p=128)
        wd = self.din(wname + "d", [FF, D]).rearrange("(fc p) d -> p fc d", p=128)
        v_gu = lambda r: r[:].rearrange("p (a b) -> p a b", a=KC)
        v_d = lambda r: r[:].rearrange("p (a b) -> p a b", a=G)

        phases = [("gate", 0), ("up", 0)]
        for g in range(1, NG):
            phases += [("gate", g), ("down", g - 1), ("up", g)]
        phases.append(("down", NG - 1))

        loads = {}

        def issue(ph):
            kind, g = ph
            if kind == "gate":
                loads[ph] = self.load_w(wg[:, :, g * 512:(g + 1) * 512], v_gu)
            elif kind == "up":
                loads[ph] = self.load_w(wu[:, :, g * 512:(g + 1) * 512], v_gu)
            else:
                loads[ph] = self.load_w(wd[:, g * G:(g + 1) * G, :], v_d)

        hn_tok = self.rmsnorm(vcol)
        hn, sg, hT, xT = self.hn, self.sg, self.hT, self.xT

        silu_done = {}
        h_done = {}
        last_pe_hn = None
        pre_sq = []
        nissued = 0
        for pi, ph in enumerate(phases):
            while nissued < len(phases) and nissued < pi + 3:
                issue(phases[nissued])
                nissued += 1
            kind, g = ph
            slot, wtok = loads[ph]
            if kind in ("gate", "up"):
                wv = v_gu(self.ring[slot])
                for fc in range(G):
                    banks = self.gu_banks()
                    for kc in range(KC):
                        for th in range(2):
                            last = (kc == KC - 1 and th == 1)
                            tok = P.op("pe", lambda e, kc=kc, th=th, fc=fc, wv=wv, banks=banks: e.matmul(
                                self.ps[banks[th]][:], wv[:, kc, fc * 128:(fc + 1) * 128], hn[:, kc, th * 512:(th + 1) * 512],
                                start=(kc == 0), stop=(kc == KC - 1)),
                                deps=[wtok, hn_tok[kc], self.bank_free[banks[0]], self.bank_free[banks[1]]],
                                sem=self.s_pe if last else None)
                    mm_tok = tok
                    last_pe_hn = tok
                    if kind == "gate":
                        for th in range(2):
                            sl = slice(th * 512, (th + 1) * 512)
                            t = P.op("act", lambda e, fc=fc, th=th, sl=sl, banks=banks: e.activation(
                                out=sg[:, fc, sl], in_=self.ps[banks[th]][:], func=AF.Silu),
                                deps=[mm_tok, self.sg_free[fc]], sem=self.s_act)
                            self.bank_free[banks[th]] = t
                        silu_done[(g, fc)] = t
                    else:
                        for th in range(2):
                            sl = slice(th * 512, (th + 1) * 512)
                            t = P.op("dve", lambda e, fc=fc, th=th, sl=sl, banks=banks: e.tensor_tensor(
                                out=hT[:, fc, sl], in0=sg[:, fc, sl], in1=self.ps[banks[th]][:], op=ALU.mult),
                                deps=[mm_tok, silu_done[(g, fc)], self.hT_free[fc]], sem=self.s_dve)
                            self.bank_free[banks[th]] = t
                        h_done[(g, fc)] = t
                        self.sg_free[fc] = t
                self.ring_free[slot] = mm_tok
            else:
                wv = v_d(self.ring[slot])
                for oc in range(KC):
                    banks = self.dn_banks()
                    for fc in range(G):
                        for th in range(2):
                            last = (fc == G - 1 and th == 1)
                            tok = P.op("pe", lambda e, oc=oc, th=th, fc=fc, wv=wv, banks=banks: e.matmul(
                                self.ps[banks[th]][:], wv[:, fc, oc * 128:(oc + 1) * 128], hT[:, fc, th * 512:(th + 1) * 512],
                                start=(fc == 0), stop=(fc == G - 1)),
                                deps=[wtok, h_done[(g, fc)], self.bank_free[banks[0]], self.bank_free[banks[1]]],
                                sem=self.s_pe if last else None)
                    mm_tok = tok
                    for th in range(2):
                        sl = slice(th * 512, (th + 1) * 512)
                        t = P.op("dve", lambda e, oc=oc, th=th, sl=sl, banks=banks: e.scalar_tensor_tensor(
                            out=xT[:, oc, sl], in0=self.ps[banks[th]][:], scalar=0.5, in1=xT[:, oc, sl],
                            op0=ALU.mult, op1=ALU.add),
                            deps=[mm_tok], sem=self.s_dve)
                        self.bank_free[banks[th]] = t
                    if g == NG - 1:
                        pre_sq.append(P.op("act", lambda e, oc=oc: e.activation(out=hn[:, oc, :], in_=xT[:, oc, :], func=AF.Square),
                                           deps=[t, last_pe_hn], sem=self.s_act))
                self.ring_free[slot] = mm_tok
                for fc in range(G):
                    self.hT_free[fc] = mm_tok
                self.x_ready = t
        self.hn_free = last_pe_hn
        self.pre_sq = pre_sq


    def rms_stats(self, src, n, out_rstd, scale, bias, sq_dst):
        P = self.P
        sq_tok = []
        for kc in range(KC):
            sq_tok.append(P.op("act", lambda e, kc=kc: e.activation(out=sq_dst[:, kc, 0:n], in_=src[:, kc, 0:n], func=AF.Square),
                               deps=[self.x_ready, self.hn_free, (self.s_ld, self.s_ld.n)], sem=self.s_act))
        bank = self.gu_banks()[0]
        for kc in range(KC):
            tok = P.op("pe", lambda e, kc=kc: e.matmul(self.ps[bank][:, 0:n], self.ones[:], sq_dst[:, kc, 0:n],
                                                       start=(kc == 0), stop=(kc == KC - 1)),
                       deps=[sq_tok[kc], self.ones_ready, self.bank_free[bank]],
                       sem=self.s_pe if kc == KC - 1 else None)
        t1 = P.op("dve", lambda e: e.tensor_scalar(out=out_rstd[:, 0:n], in0=self.ps[bank][:, 0:n], scalar1=scale, scalar2=bias,
                                                   op0=ALU.mult, op1=ALU.add), deps=[tok], sem=self.s_dve)
        self.bank_free[bank] = t1
        t15 = P.op("act", lambda e: e.activation(out=out_rstd[:, 0:n], in_=out_rstd[:, 0:n], func=AF.Sqrt), deps=[t1], sem=self.s_act)
        t2 = P.op("dve", lambda e: e.reciprocal(out=out_rstd[:, 0:n], in_=out_rstd[:, 0:n]), deps=[t15], sem=self.s_dve)
        self.hn_free = tok
        return t2


    def halo_exchange(self):
        self.halo_start()
        self.halo_finish()

    def halo_start(self):
        self.cc_write(0, lambda e, src: e.dma_start(
            out=src.ap().bitcast(F32)[:, 0:256].rearrange("p (a b) -> p a b", a=KC), in_=self.xT[:, :, T - 16:T]), [self.x_ready])
        ctok = self.cc_go(0)

        def rd(e, dst):
            prev = (self.pid_sp + 3) % 4
            return e.dma_start(out=self.halo[:], in_=dst.ap().bitcast(F32).rearrange("(r p) c -> r p c", r=4)[bass.ds(prev, 1), :, 0:256]
                               .rearrange("r p (a b) -> p (r a) b", a=KC))
        self.cc_read(0, rd, ctok, eng="sp")

    def halo_finish(self):
        P = self.P
        P.wait_only("dve", [(self.s_cr[0], self.s_cr[0].n)])
        self.halo_tok = P.op("dve", lambda e: e.tensor_scalar(out=self.halo[:], in0=self.halo[:], scalar1=self.vecs[:, V_HF:V_HF + 1], scalar2=None, op0=ALU.mult),
                             deps=[self.vecs_ready], sem=self.s_dve)
        P.wait_only("act", [self.halo_tok])

    def st_halo_dbg(self):
        sb = self.es_sb
        self.barrier()
        if not hasattr(self, "halo"):
            self.halo = sb("halo", [128, KC, 16], F32)
        self.halo_exchange()
        P = self.P
        out = self.dout("halo_dbg", [128, KC, 16])
        self.P.op("sp", lambda e: e.dma_start(out=out, in_=self.halo[:]), deps=[(self.s_dve, self.s_dve.n)], sem=self.s_st, by=16)

    def st_pool(self, j, layer):
        P = self.P
        sb = self.es_sb
        xT, hn = self.xT, self.hn
        L = T + 16
        self.barrier()
        if not hasattr(self, "halo"):
            self.halo = sb("halo", [128, KC, 16], F32)
            self.halo_sq = sb("halo_sq", [128, KC, 16], BF16)
            self.rstd_h = sb("rstd_h", [128, 16], F32)
            self.invc = sb("invc", [128, 4, 16], F32)
        self.hp = self.sgh_f32[:, 0:L]
        self.wa = self.sgh_f32[:, L:2 * L]
        self.wb = self.sgh_f32[:, 2 * L:3 * L]
        invc_d = self.din("invc", [128, 4, 16])
        if not self.fused:
            halo_d = self.din(f"halo{layer}", [128, KC, 16])
            P.op("sp", lambda e: e.dma_start(out=self.halo[:], in_=halo_d), sem=self.s_ld, by=16)
        else:
            self.halo_start()
        P.op("sp", lambda e: e.dma_start(out=self.invc[:], in_=invc_d), sem=self.s_ld, by=16)
        ld_tok = (self.s_ld, self.s_ld.n)
        pw = self.din(f"poolw{j}", [4, 512, 512]).rearrange("g (c p) d -> p g c d", p=128)
        slot, wtok = self.load_w(pw, lambda r: r[:].rearrange("p (g c d) -> p g c d", g=4, c=4))
        wv = self.ring[slot][:].rearrange("p (g c d) -> p g c d", g=4, c=4)
        vcol = V_MX + layer * KC
        r_main = self.rmsnorm_stats_main()
        if self.fused:
            self.halo_finish()
        r_halo = self.rms_stats(self.halo, 16, self.rstd_h, 1.0 / D, EPS, self.halo_sq)
        hp, wa, wb = self.hp, self.wa, self.wb
        prev_read = None
        for kc in range(KC):
            gi = kc // 4
            w = 2 ** (gi + 1)
            gcol = self.vecs[:, vcol + kc:vcol + kc + 1]
            a = P.op("dve", lambda e, kc=kc, gcol=gcol: e.scalar_tensor_tensor(out=hp[:, 16:L], in0=xT[:, kc, :], scalar=gcol, in1=self.rstd[:],
                                                                              op0=ALU.mult, op1=ALU.mult),
                     deps=[r_main, self.vecs_ready, ld_tok, prev_read], sem=self.s_dve)
            b = P.op("dve", lambda e, kc=kc, gcol=gcol: e.scalar_tensor_tensor(out=hp[:, 0:16], in0=self.halo[:, kc, :], scalar=gcol, in1=self.rstd_h[:],
                                                                              op0=ALU.mult, op1=ALU.mult),
                     deps=[r_halo, a], sem=self.s_dve)
            src = hp
            last = b
            for sidx in range(gi + 1):
                sh = 2 ** sidx
                dst = wa if sidx % 2 == 0 else wb
                last = P.op("dve", lambda e, src=src, dst=dst, sh=sh: e.tensor_tensor(out=dst[:, sh:L], in0=src[:, sh:L], in1=src[:, 0:L - sh], op=ALU.add),
                            deps=[last], sem=self.s_dve)
                src = dst
            c1 = P.op("dve", lambda e, kc=kc, src=src, w=w: e.scalar_tensor_tensor(out=hn[:, kc, :], in0=src[:, 16:L], scalar=1.0 / w, in1=hp[:, 16:L],
                                                                                   op0=ALU.mult, op1=ALU.subtract),
                      deps=[last, self.hn_free], sem=self.s_dve)
            tmp = wb if src is wa else wa
            c2 = P.op("dve", lambda e, src=src, gi=gi, tmp=tmp: e.tensor_tensor(out=tmp[:, 0:16], in0=src[:, 16:32], in1=self.invc[:, gi, :], op=ALU.mult),
                      deps=[c1], sem=self.s_dve)
            c3 = P.op("dve", lambda e, kc=kc, tmp=tmp: e.tensor_tensor(out=hn[:, kc, 0:16], in0=tmp[:, 0:16], in1=hp[:, 16:32], op=ALU.subtract),
                      deps=[c2], sem=self.s_dve)
            prev_read = c3
        diff_tok = prev_read
        pscol = V_PS + j * KC
        for g in range(4):
            for oc in range(4):
                banks = self.dn_banks()
                for c in range(4):
                    for th in range(2):
                        last = (c == 3 and th == 1)
                        tok = P.op("pe", lambda e, g=g, oc=oc, c=c, th=th, banks=banks: e.matmul(
                            self.ps[banks[th]][:], wv[:, g, c, oc * 128:(oc + 1) * 128], hn[:, g * 4 + c, th * 512:(th + 1) * 512],
                            start=(c == 0), stop=(c == 3)),
                            deps=[wtok, diff_tok, self.bank_free[banks[0]], self.bank_free[banks[1]]],
                            sem=self.s_pe if last else None)
                mm_tok = tok
                o = g * 4 + oc
                for th in range(2):
                    sl = slice(th * 512, (th + 1) * 512)
                    t = P.op("dve", lambda e, o=o, th=th, sl=sl, banks=banks: e.scalar_tensor_tensor(
                        out=xT[:, o, sl], in0=self.ps[banks[th]][:], scalar=self.vecs[:, pscol + o:pscol + o + 1], in1=xT[:, o, sl],
                        op0=ALU.mult, op1=ALU.add), deps=[mm_tok], sem=self.s_dve)
                    self.bank_free[banks[th]] = t
        self.ring_free[slot] = mm_tok
        self.hn_free = mm_tok
        self.x_ready = t
        self.pre_sq = None

    def rmsnorm_stats_main(self):
        P = self.P
        xT, hn, rstd = self.xT, self.hn, self.rstd
        if self.pre_sq:
            sq_tok = self.pre_sq
        else:
            sq_tok = []
            for kc in range(KC):
                sq_tok.append(P.op("act", lambda e, kc=kc: e.activation(out=hn[:, kc, :], in_=xT[:, kc, :], func=AF.Square),
                                   deps=[self.x_ready, self.hn_free], sem=self.s_act))
        self.pre_sq = None
        banks = self.gu_banks()
        for th in range(2):
            for kc in range(KC):
                last = kc == KC - 1
                tok = P.op("pe", lambda e, th=th, kc=kc: e.matmul(self.ps[banks[th]][:], self.ones[:], hn[:, kc, th * 512:(th + 1) * 512],
                                                                   start=(kc == 0), stop=(kc == KC - 1)),
                           deps=[sq_tok[kc], self.ones_ready, self.bank_free[banks[th]]],
                           sem=self.s_pe if last else None)
        ssq_tok = tok
        for th in range(2):
            sl = slice(th * 512, (th + 1) * 512)
            t1 = P.op("dve", lambda e, th=th, sl=sl: e.tensor_scalar(out=rstd[:, sl], in0=self.ps[banks[th]][:], scalar1=1.0 / D, scalar2=EPS,
                                                                      op0=ALU.mult, op1=ALU.add),
                      deps=[ssq_tok, self.rstd_free], sem=self.s_dve)
            self.bank_free[banks[th]] = t1
            t15 = P.op("act", lambda e, sl=sl: e.activation(out=rstd[:, sl], in_=rstd[:, sl], func=AF.Sqrt), deps=[t1], sem=self.s_act)
            t2 = P.op("dve", lambda e, sl=sl: e.reciprocal(out=rstd[:, sl], in_=rstd[:, sl]), deps=[t15], sem=self.s_dve)
        self.hn_free = ssq_tok
        return t2

    def st_fox_pre(self, layer):
        P = self.P
        toks = self.rmsnorm(V_MX + layer * KC)
        tok = toks[-1]
        if not self.fused:
            hn_out = self.dout("hn_out", [128, KC, T], BF16)
            for i in range(2):
                P.op("sp", lambda e, i=i: e.dma_start(out=hn_out[:, 8 * i:8 * i + 8, :], in_=self.hn[:, 8 * i:8 * i + 8, :]),
                     deps=[tok], sem=self.s_st, by=16)
            self.hn_free = (self.s_st, self.s_st.n)
            return
        hv = self.hn_all_d.rearrange("p kc (r t) -> r p kc t", r=4)

        def rd_all(pk, pi, pc):
            for r in range(4):
                self.cc_read(pi, lambda e, dst, pk=pk, r=r: e.dma_start(
                    out=hv[r, :, 2 * pk:2 * pk + 2, :],
                    in_=dst.ap()[r * 128:(r + 1) * 128, :].rearrange("p (a b) -> p a b", a=2)), pc)

        prev = None
        for k in range(8):
            i = k % 2
            self.cc_write(i, lambda e, src, k=k: e.dma_start(out=src.ap().rearrange("p (a b) -> p a b", a=2), in_=self.hn[:, 2 * k:2 * k + 2, :]), [toks[2 * k + 1]])
            ctok = self.cc_go(i)
            if prev is not None:
                rd_all(*prev)
            prev = (k, i, ctok)
        rd_all(*prev)
        self.hn_free = (self.s_cw[1], self.s_cw[1].n)

    def st_fox_post(self, jf):
        P = self.P
        hn, xT = self.hn, self.xT
        self.barrier()
        if not self.fused:
            o_in = self.din("oT_in", [128, KC, T], BF16)
            for i in range(2):
                P.op("sp", lambda e, i=i: e.dma_start(out=hn[:, 8 * i:8 * i + 8, :], in_=o_in[:, 8 * i:8 * i + 8, :]),
                     deps=[self.hn_free], sem=self.s_ld, by=16)
        else:
            for i in range(2):
                P.op("sp", lambda e, i=i: e.dma_start(out=hn[:, 8 * i:8 * i + 8, :],
                                                      in_=self.oT_all_d[:, 8 * i:8 * i + 8, bass.ds((self.pid_sp % 4) * T, T)]),
                     deps=[self.hn_free], sem=self.s_ld, by=16)
        o_tok = (self.s_ld, self.s_ld.n)
        wo = self.din(f"foxwo{jf}", [D, D]).rearrange("(h p) o -> p h o", p=128)
        v_gu = lambda r: r[:].rearrange("p (a b) -> p a b", a=KC)
        loads = [self.load_w(wo[:, :, r * 512:(r + 1) * 512], v_gu) for r in range(4)]
        for r in range(4):
            slot, wtok = loads[r]
            wv = v_gu(self.ring[slot])
            for oc4 in range(4):
                oc = r * 4 + oc4
                banks = self.dn_banks()
                for h in range(KC):
                    for th in range(2):
                        last = (h == KC - 1 and th == 1)
                        tok = P.op("pe", lambda e, h=h, th=th, oc4=oc4, wv=wv, banks=banks: e.matmul(
                            self.ps[banks[th]][:], wv[:, h, oc4 * 128:(oc4 + 1) * 128], hn[:, h, th * 512:(th + 1) * 512],
                            start=(h == 0), stop=(h == KC - 1)),
                            deps=[wtok, o_tok, self.bank_free[banks[0]], self.bank_free[banks[1]]],
                            sem=self.s_pe if last else None)
                mm_tok = tok
                for th in range(2):
                    sl = slice(th * 512, (th + 1) * 512)
                    t = P.op("dve", lambda e, oc=oc, th=th, sl=sl, banks=banks: e.tensor_tensor(
                        out=xT[:, oc, sl], in0=self.ps[banks[th]][:], in1=xT[:, oc, sl], op=ALU.add),
                        deps=[mm_tok, self.x_ready], sem=self.s_dve)
                    self.bank_free[banks[th]] = t
            self.ring_free[slot] = mm_tok
        self.hn_free = mm_tok
        self.x_ready = t

    def st_fox_attn(self, jf):
        P = self.P
        sb = self.es_sb
        NT = SEQ // 512
        NKT = SEQ // 128
        self.barrier()
        hn_all = self.hn_all_d if self.fused else self.din("hn_all", [128, KC, SEQ], BF16)
        wq = self.din(f"wq{jf}", [D, NH * DH]).rearrange("(kc p) f -> p kc f", p=128)
        wk = self.din(f"wk{jf}", [D, NH * DH]).rearrange("(kc p) f -> p kc f", p=128)
        wvd = self.din(f"wv{jf}", [D, NH * DH]).rearrange("(kc p) f -> p kc f", p=128)
        wf_d = self.din(f"wf96_{jf}", [D, 96]).rearrange("(kc p) f -> p kc f", p=128)
        b96_d = self.din(f"b96_{jf}", [96, 1])
        sel_d = self.din("sel", [96, NH, 128], BF16)
        id96_d = self.din("id96", [96, NH])
        mask_d = self.din("maskneg", [128, 4, 512], BF16)
        idb_d = self.din("identb", [128, 128], BF16)
        if not self.fused:
            oT_out = self.dout("oT_out", [128, NH, SEQ], BF16)
        if not hasattr(self, "qT_t"):
            self.qT_t = sb("qT", [128, SEQ], BF16)
            self.b96_t = sb("b96_sb", [96, 1], F32)
            self.sel_t = sb("sel_sb", [96, NH, 128], BF16)
            self.id96_t = sb("id96_sb", [96, NH], F32)
            self.mask_t = sb("mask_sb", [128, 4, 512], BF16)
            self.identb_t = sb("identb_sb", [128, 128], BF16)
            self.A96_t = sb("A96", [96, SEQ], BF16)
            self.negF_t = sb("negF", [128, NKT * NH], F32)
        qT = self.qT_t
        b96, sel, id96, maskneg, identb, A96, negF = self.b96_t, self.sel_t, self.id96_t, self.mask_t, self.identb_t, self.A96_t, self.negF_t
        wf = qT[:, 0:KC * 96].rearrange("p (a b) -> p a b", a=KC)
        TB = qT[0:96, :]
        Z = self.hn_f32[0:96, 0:SEQ]
        T1 = self.hn_f32[0:96, SEQ:2 * SEQ]
        F96 = self.sgh_f32[0:96, :]
        kT = self.sgh_flat[:, 0:SEQ]
        vv = self.sgh_flat[:, SEQ:2 * SEQ].rearrange("p (a b) -> p a b", a=NKT)
        hf = self.hn_flat
        wqkv = hf[:, 0:6144].rearrange("p (w a b) -> p w a b", w=3, a=KC)
        sqb = [hf[:, 6144:6656], hf[:, 6656:7168]]
        pT = [hf[:, 7168 + 512 * i:7168 + 512 * (i + 1)] for i in range(4)]
        obuf = [hf[:, 9216 + 512 * i:9216 + 512 * (i + 1)] for i in range(4)]
        rden = hf[:, 11264:12288].bitcast(F32)
        rb = [self.rstd[:, 0:512], self.rstd[:, 512:1024]]

        v_gu = lambda r: r[:].rearrange("p (a b) -> p a b", a=KC)
        for dst, src in ((b96, b96_d), (sel, sel_d), (id96, id96_d), (maskneg, mask_d), (identb, idb_d)):
            P.op("sp", lambda e, dst=dst, src=src: e.dma_start(out=dst[:], in_=src), sem=self.s_ld, by=16)
        const_tok = (self.s_ld, self.s_ld.n)
        wf_tok = P.op("pool", lambda e: e.dma_start(out=wf[:], in_=wf_d), sem=self.s_ld, by=16)
        wf_tok = (self.s_ld, self.s_ld.n)

        ztok = None
        for tt in range(NT):
            slot, htok = self.load_w(hn_all[:, :, tt * 512:(tt + 1) * 512], v_gu, eng="sp")
            hv = v_gu(self.ring[slot])
            bank = tt % 2
            for kc in range(KC):
                tok = P.op("pe", lambda e, kc=kc, hv=hv, bank=bank: e.matmul(self.ps[bank][0:96, :], wf[:, kc, :], hv[:, kc, :],
                                                                            start=(kc == 0), stop=(kc == KC - 1)),
                           deps=[htok, wf_tok, self.bank_free[bank]], sem=self.s_pe if kc == KC - 1 else None)
            self.ring_free[slot] = tok
            ztok = P.op("dve", lambda e, tt=tt, bank=bank: e.tensor_scalar(out=Z[:, tt * 512:(tt + 1) * 512], in0=self.ps[bank][0:96, :],
                                                                          scalar1=b96[:, 0:1], scalar2=None, op0=ALU.add),
                        deps=[tok, const_tok], sem=self.s_dve)
            self.bank_free[bank] = ztok
        t = P.op("act", lambda e: e.activation(out=T1[:], in_=Z[:], func=AF.Abs), deps=[ztok], sem=self.s_act)
        t = P.op("act", lambda e: e.activation(out=T1[:], in_=T1[:], func=AF.Exp, scale=-1.0), deps=[t], sem=self.s_act)
        t = P.op("dve", lambda e: e.tensor_scalar(out=T1[:], in0=T1[:], scalar1=1.0, scalar2=None, op0=ALU.add), deps=[t], sem=self.s_dve)
        t = P.op("act", lambda e: e.activation(out=T1[:], in_=T1[:], func=AF.Ln), deps=[t], sem=self.s_act)
        t2 = P.op("dve", lambda e: e.tensor_single_scalar(out=Z[:], in_=Z[:], scalar=0.0, op=ALU.min), deps=[t], sem=self.s_dve)
        t = P.op("dve", lambda e: e.tensor_tensor(out=Z[:], in0=Z[:], in1=T1[:], op=ALU.subtract), deps=[t2, t], sem=self.s_dve)
        t = P.op("dve", lambda e: e.tensor_tensor_scan(out=F96[:], data0=self.ones[0:96, 0:1].to_broadcast([96, SEQ]), data1=Z[:],
                                                       initial=0.0, op0=ALU.mult, op1=ALU.add),
                 deps=[t, self.ones_ready], sem=self.s_dve)
        f_tok = t
        t = P.op("dve", lambda e: e.tensor_copy(out=A96[:], in_=F96[:]), deps=[t], sem=self.s_dve)
        t = P.op("dve", lambda e: e.tensor_tensor(out=T1[:], in0=F96[:], in1=A96[:], op=ALU.subtract), deps=[t], sem=self.s_dve)
        t = P.op("dve", lambda e: e.tensor_copy(out=TB[:], in_=T1[:]), deps=[t], sem=self.s_dve)
        t = P.op("dve", lambda e: e.tensor_tensor(out=T1[64:96, :], in0=T1[64:96, :], in1=TB[64:96, :], op=ALU.subtract), deps=[t], sem=self.s_dve)
        t = P.op("dve", lambda e: e.tensor_copy(out=A96[32:64, :], in_=TB[32:64, :]), deps=[t], sem=self.s_dve)
        t = P.op("dve", lambda e: e.tensor_copy(out=A96[64:96, :], in_=T1[64:96, :]), deps=[t], sem=self.s_dve)
        a_tok = t
        for kt in range(NKT):
            tok = P.op("pe", lambda e, kt=kt: e.matmul(self.ps[2][:, kt * NH:(kt + 1) * NH], F96[:, kt * 128:(kt + 1) * 128], id96[:],
                                                       start=True, stop=True),
                       deps=[f_tok, const_tok, self.bank_free[2]], sem=self.s_pe if kt == NKT - 1 else None)
        t = P.op("dve", lambda e: e.tensor_scalar(out=negF[:], in0=self.ps[2][:, 0:NKT * NH], scalar1=-1.0, scalar2=None, op0=ALU.mult),
                 deps=[tok], sem=self.s_dve)
        self.bank_free[2] = t
        negf_tok = t

        self.barrier()
        pending_rd = None
        next_w_tok = None
        qv_free = None
        pT_free = [None] * 4
        obuf_free = [None] * 4
        rden_free = None
        p_rr = 0
        s_rr = 0
        o_rr = 0
        qcol = V_QG + jf
        kcol = V_KG + jf
        for h in range(NH):
            if h == 0:
                for wi, wsrc in enumerate((wq, wk, wvd)):
                    P.op("pool", lambda e, wi=wi, wsrc=wsrc: e.dma_start(out=wqkv[:, wi, :, :], in_=wsrc[:, :, 0:DH]),
                         deps=[(self.s_pe, self.s_pe.n)], sem=self.s_ld, by=16)
                w_tok = (self.s_ld, self.s_ld.n)
            else:
                w_tok = next_w_tok
            qk_ready = None
            for tt in range(NT):
                slot, htok = self.load_w(hn_all[:, :, tt * 512:(tt + 1) * 512], v_gu, eng="sp")
                hv = v_gu(self.ring[slot])
                mm = {}
                qkb = (0, 1) if tt % 2 == 0 else (5, 6)
                for wi, bank in ((0, qkb[0]), (1, qkb[1])):
                    for kc in range(KC):
                        tok = P.op("pe", lambda e, kc=kc, hv=hv, wi=wi, bank=bank: e.matmul(self.ps[bank][:], wqkv[:, wi, kc, :], hv[:, kc, :],
                                                                                           start=(kc == 0), stop=(kc == KC - 1)),
                                   deps=[htok, w_tok, self.bank_free[bank]], sem=self.s_pe if kc == KC - 1 else None)
                    mm[wi] = tok
                for s4 in range(4):
                    for kc in range(KC):
                        tok = P.op("pe", lambda e, kc=kc, hv=hv, s4=s4: e.matmul(self.ps[2][:, s4 * DH:(s4 + 1) * DH], hv[:, kc, s4 * 128:(s4 + 1) * 128],
                                                                                 wqkv[:, 2, kc, :], start=(kc == 0), stop=(kc == KC - 1)),
                                   deps=[htok, w_tok, self.bank_free[2]], sem=self.s_pe if (kc == KC - 1 and s4 == 3) else None)
                mm[2] = tok
                self.ring_free[slot] = tok
                tv = P.op("act", lambda e, tt=tt: e.activation(out=vv[:, tt * 4:(tt + 1) * 4, :], in_=self.ps[2][:].rearrange("p (a b) -> p a b", a=4),
                                                              func=AF.Copy), deps=[mm[2], qv_free], sem=self.s_act)
                self.bank_free[2] = tv
                sq_t = {}
                for wi in (0, 1):
                    sq_t[wi] = P.op("act", lambda e, wi=wi, qkb=qkb: e.activation(out=sqb[wi][:], in_=self.ps[qkb[wi]][:], func=AF.Square),
                                    deps=[mm[wi], self.bank_free[3 + wi]], sem=self.s_act)
                ss_t = {}
                for wi in (0, 1):
                    ss_t[wi] = P.op("pe", lambda e, wi=wi: e.matmul(self.ps[3 + wi][:], self.ones[:], sqb[wi][:], start=True, stop=True),
                                    deps=[sq_t[wi], self.ones_ready, self.bank_free[3 + wi]], sem=self.s_pe)
                for wi, dst, gcol in ((0, qT, qcol), (1, kT, kcol)):
                    sc, bi = (1.0, DH * EPS) if wi == 0 else (1.0 / DH, EPS)
                    t1 = P.op("dve", lambda e, wi=wi, sc=sc, bi=bi: e.tensor_scalar(out=rb[wi][:], in0=self.ps[3 + wi][:], scalar1=sc, scalar2=bi,
                                                                                    op0=ALU.mult, op1=ALU.add), deps=[ss_t[wi]], sem=self.s_dve)
                    self.bank_free[3 + wi] = t1
                    t15 = P.op("act", lambda e, wi=wi: e.activation(out=rb[wi][:], in_=rb[wi][:], func=AF.Sqrt), deps=[t1], sem=self.s_act)
                    t2 = P.op("dve", lambda e, wi=wi: e.reciprocal(out=rb[wi][:], in_=rb[wi][:]), deps=[t15], sem=self.s_dve)
                    t3 = P.op("dve", lambda e, wi=wi, dst=dst, gcol=gcol, tt=tt, qkb=qkb: e.scalar_tensor_tensor(
                        out=dst[:, tt * 512:(tt + 1) * 512], in0=self.ps[qkb[wi]][:], scalar=self.vecs[:, gcol:gcol + 1], in1=rb[wi][:],
                        op0=ALU.mult, op1=ALU.mult), deps=[t2, self.vecs_ready, qv_free], sem=self.s_dve)
                    self.bank_free[qkb[wi]] = t3
                qk_ready = t3
            v_ready = tv
            if h + 1 < NH:
                for wi, wsrc in enumerate((wq, wk, wvd)):
                    P.op("pool", lambda e, wi=wi, wsrc=wsrc, h=h: e.dma_start(out=wqkv[:, wi, :, :], in_=wsrc[:, :, (h + 1) * DH:(h + 2) * DH]),
                         deps=[(self.s_pe, self.s_pe.n)], sem=self.s_ld, by=16)
                next_w_tok = (self.s_ld, self.s_ld.n)
            last_pv = None
            for qt in range(NT):
                nkt = (qt + 1) * 4
                ob, db = (0, 1) if o_rr % 2 == 0 else (2, 3)
                o_rr += 1
                pinfo = {}

                def emit_S(kt, qt=qt, h=h):
                    nonlocal s_rr, p_rr
                    sbk = 4 + (s_rr % 4)
                    s_rr += 1
                    pi = p_rr % 4
                    p_rr += 1
                    diag = kt >= qt * 4
                    P.op("pe", lambda e: e.matmul(self.ps[sbk][:], kT[:, kt * 128:(kt + 1) * 128], qT[:, qt * 512:(qt + 1) * 512],
                                                  start=True, stop=False),
                         deps=[qk_ready, self.bank_free[sbk]])
                    tok = P.op("pe", lambda e: e.matmul(self.ps[sbk][:], sel[:, h, :], A96[:, qt * 512:(qt + 1) * 512],
                                                        start=False, stop=(not diag)),
                               deps=[a_tok, const_tok], sem=None if diag else self.s_pe)
                    if diag:
                        o = kt - qt * 4
                        tok = P.op("pe", lambda e: e.matmul(self.ps[sbk][:], identb[:], maskneg[:, o, :], start=False, stop=True),
                                   deps=[const_tok], sem=self.s_pe)
                    col = kt * NH + h
                    pt = P.op("act", lambda e: e.activation(out=pT[pi][:], in_=self.ps[sbk][:], func=AF.Exp, bias=negF[:, col:col + 1]),
                              deps=[tok, negf_tok, pT_free[pi]], sem=self.s_act)
                    self.bank_free[sbk] = pt
                    pinfo[kt] = (pi, pt)

                def emit_PV(kt, qt=qt, nkt=nkt, ob=ob, db=db):
                    pi, pt = pinfo[kt]
                    P.op("pe", lambda e: e.matmul(self.ps[ob][:], vv[:, kt, :], pT[pi][:], start=(kt == 0), stop=(kt == nkt - 1)),
                         deps=[pt, v_ready, self.bank_free[ob]])
                    tok = P.op("pe", lambda e: e.matmul(self.ps[db][:], self.ones[:], pT[pi][:], start=(kt == 0), stop=(kt == nkt - 1)),
                               deps=[self.bank_free[db], self.ones_ready], sem=self.s_pe)
                    pT_free[pi] = tok
                    return tok

                emit_S(0)
                emit_S(1)
                emit_S(2)
                for kt in range(nkt):
                    last_pv = emit_PV(kt)
                    if kt + 3 < nkt:
                        emit_S(kt + 3)
                oi = (qt % 4) if self.fused else (qt % 2)
                t1 = P.op("dve", lambda e, db=db: e.reciprocal(out=rden[:], in_=self.ps[db][:]), deps=[last_pv, rden_free], sem=self.s_dve)
                t2 = P.op("dve", lambda e, ob=ob, oi=oi: e.tensor_tensor(out=obuf[oi][:], in0=self.ps[ob][:], in1=rden[:], op=ALU.mult),
                          deps=[t1, obuf_free[oi]], sem=self.s_dve)
                rden_free = t2
                self.bank_free[ob] = t2
                self.bank_free[db] = t2
                if not self.fused:
                    P.op("sp", lambda e, oi=oi, h=h, qt=qt: e.dma_start(out=oT_out[:, h, qt * 512:(qt + 1) * 512], in_=obuf[oi][:]),
                         deps=[t2], sem=self.s_ob[oi], by=16)
                    obuf_free[oi] = (self.s_ob[oi], self.s_ob[oi].n)
                else:
                    ci = (qt // 4) % 2
                    self.cc_write(ci, lambda e, src, oi=oi, qt=qt: e.dma_start(out=src.ap()[:, (qt % 4) * 512:(qt % 4 + 1) * 512], in_=obuf[oi][:]), [t2])
                    if qt % 4 == 3:
                        for k4 in range(4):
                            obuf_free[k4] = (self.s_cw[ci], self.s_cw[ci].n)
                        if pending_rd is not None:
                            self.cc_read(*pending_rd)
                        ctok = self.cc_go(ci)
                        hfi = qt // 4
                        ov = self.oT_all_d.rearrange("p (hg hh) t -> hg p hh t", hg=4)
                        pending_rd = (ci, lambda e, dst, h=h, hfi=hfi: e.dma_start(out=ov[:, :, h, hfi * 2048:(hfi + 1) * 2048],
                                                                                  in_=dst.ap().rearrange("(r p) t -> r p t", r=4)), ctok)
            qv_free = last_pv
        if pending_rd is not None:
            self.cc_read(*pending_rd)
        self.barrier()
        self.hn_free = None

def to_fm(x):
    xs = x.reshape(8, T, KC, 128)
    return [np.ascontiguousarray(xs[c].transpose(2, 1, 0)) for c in range(8)]


def from_fm(xts):
    out = np.empty((8, T, KC, 128), np.float32)
    for c in range(8):
        out[c] = xts[c].transpose(2, 1, 0)
    return out.reshape(2, SEQ, D)


def make_vecs(inp, core=0):
    v = np.zeros((128, NV), np.float32)
    v[:, V_HF] = 0.0 if core % 4 == 0 else 1.0
    for l in range(DEPTH):
        v[:, V_F1 + l * KC:V_F1 + (l + 1) * KC] = inp["ffn1_norm"][l].reshape(KC, 128).T
        v[:, V_F2 + l * KC:V_F2 + (l + 1) * KC] = inp["ffn2_norm"][l].reshape(KC, 128).T
        v[:, V_MX + l * KC:V_MX + (l + 1) * KC] = inp["mix_norm"][l].reshape(KC, 128).T
    for j in range(2):
        v[:, V_PS + j * KC:V_PS + (j + 1) * KC] = inp["pool_scale"][j].reshape(KC, 128).T
        v[:, V_QG + j] = inp["fox_q_gain"][j]
        v[:, V_KG + j] = inp["fox_k_gain"][j]
    return v


_cache = {}
FUSED = True


def run_stage_list(stages, in_maps, trace=False):
    key = repr(stages)
    if key not in _cache:
        b = Builder(stages)
        nc = b.build()
        _cache[key] = (nc, b)
    nc, b = _cache[key]
    maps = [{k: m[k] for k in b.dram_in} for m in in_maps]
    res = run_bass_kernel_spmd(nc, maps, core_ids=list(range(8)), trace=trace)
    return res


BF = ml_dtypes.bfloat16
POOL_WINDOWS = (2, 4, 8, 16)


def _const_tables():
    sel = np.zeros((96, NH, 128), np.float32)
    id96 = np.zeros((96, NH), np.float32)
    for h in range(NH):
        for grp in range(3):
            sel[32 * grp + h, h, :] = 1.0
        id96[h, h] = 1.0
    p = np.arange(128)[:, None, None]
    o = np.arange(4)[None, :, None]
    c = np.arange(512)[None, None, :]
    maskneg = np.where(o * 128 + p <= c, 0.0, -30000.0).astype(np.float32)
    identb = np.eye(128, dtype=np.float32)
    return sel.astype(BF), id96, maskneg.astype(BF), identb.astype(BF)


def _invc(j):
    t = np.arange(16)
    out = np.zeros((128, 4, 16), np.float32)
    for g, w in enumerate(POOL_WINDOWS):
        cnt = np.minimum(t + 1, w) if j == 0 else np.full(16, w)
        out[:, g, :] = (1.0 / cnt.astype(np.float32))[None, :]
    return out


def kernel(**inp):
    inp = {k: np.asarray(v) for k, v in inp.items()}
    x = inp["x"].astype(np.float32, copy=False).reshape(8 * T, D)
    sel, id96, maskneg, identb = _const_tables()
    base = []
    for c in range(8):
        m = {"vecs": make_vecs(inp, c), "invc": _invc(c % 4), "sel": sel, "id96": id96, "maskneg": maskneg, "identb": identb}
        for l in range(DEPTH):
            for nm, pre in (("f1_", "ffn1_"), ("f2_", "ffn2_")):
                m[f"{nm}{l}g"] = inp[pre + "w_gate"][l]
                m[f"{nm}{l}u"] = inp[pre + "w_up"][l]
                m[f"{nm}{l}d"] = inp[pre + "w_down"][l]
        for j in range(2):
            m[f"poolw{j}"] = inp["pool_w"][j]
            m[f"foxwo{j}"] = inp["fox_w_out"][j]
        base.append(m)

    def fox_inputs(jf, maps):
        win = inp["fox_w_in"][jf]
        for c in range(8):
            hg = c % 4
            maps[c][f"wq{jf}"] = np.ascontiguousarray(win[:, hg * 512:(hg + 1) * 512])
            maps[c][f"wk{jf}"] = np.ascontiguousarray(win[:, D + hg * 512:D + (hg + 1) * 512])
            maps[c][f"wv{jf}"] = np.ascontiguousarray(win[:, 2 * D + hg * 512:2 * D + (hg + 1) * 512])
            wf96 = np.zeros((D, 96), np.float32)
            b96 = np.zeros((96, 1), np.float32)
            for h in range(NH):
                for grp in range(3):
                    wf96[:, 32 * grp + h] = win[:, 3 * D + hg * NH + h]
                    b96[32 * grp + h, 0] = inp["fox_b_f"][jf][hg * NH + h]
            maps[c][f"wf96_{jf}"] = wf96
            maps[c][f"b96_{jf}"] = b96

    def halos(xts):
        out = []
        for c in range(8):
            if c % 4 == 0:
                out.append(np.zeros((128, KC, 16), np.float32))
            else:
                out.append(np.ascontiguousarray(xts[c - 1][:, :, T - 16:T]))
        return out

    def run(stages, extra):
        maps = [dict(base[c], **extra[c]) for c in range(8)]
        return run_stage_list(stages, maps).results

    def attn_round(jf, hn_outs):
        extra = [dict() for _ in range(8)]
        for b in range(2):
            hn_all = np.ascontiguousarray(np.concatenate([hn_outs[b * 4 + j] for j in range(4)], axis=2))
            for j in range(4):
                extra[b * 4 + j]["hn_all"] = hn_all
        fox_inputs(jf, extra)
        r = run([("fox_attn", jf)], extra)
        o_ins = []
        for c in range(8):
            b, j = divmod(c, 4)
            o_ins.append(np.ascontiguousarray(np.concatenate(
                [r[b * 4 + hg]["oT_out"][:, :, j * T:(j + 1) * T] for hg in range(4)], axis=1)))
        return o_ins

    xts = to_fm(x)
    if FUSED:
        extra = [{"xT_in": xts[c]} for c in range(8)]
        fox_inputs(0, extra)
        fox_inputs(1, extra)
        stages = [("fused",), ("load_x",), ("ffn", "f1_0", 0, V_F1)]
        for rnd in range(2):
            lp, lf = 2 * rnd, 2 * rnd + 1
            stages += [("pool", rnd, lp), ("ffn", f"f2_{lp}", lp, V_F2 + lp * KC), ("ffn", f"f1_{lf}", lf, V_F1 + lf * KC),
                       ("fox_pre", lf), ("fox_attn", rnd), ("fox_post", rnd), ("ffn", f"f2_{lf}", lf, V_F2 + lf * KC)]
            if rnd == 0:
                stages.append(("ffn", "f1_2", 2, V_F1 + 2 * KC))
        stages.append(("store_x",))
        r = run(stages, extra)
        return from_fm([r[c]["xT_out"] for c in range(8)]).astype(np.float32)
    r = run([("load_x",), ("ffn", "f1_0", 0, V_F1 + 0 * KC), ("store_x",)], [{"xT_in": xts[c]} for c in range(8)])
    xts = [r[c]["xT_out"] for c in range(8)]
    for rnd in range(2):
        lp = 2 * rnd
        lf = 2 * rnd + 1
        hs = halos(xts)
        r = run([("load_x",), ("pool", rnd, lp), ("ffn", f"f2_{lp}", lp, V_F2 + lp * KC),
                 ("ffn", f"f1_{lf}", lf, V_F1 + lf * KC), ("fox_pre", lf), ("store_x",)],
                [{"xT_in": xts[c], f"halo{lp}": hs[c]} for c in range(8)])
        xts = [r[c]["xT_out"] for c in range(8)]
        o_ins = attn_round(rnd, [r[c]["hn_out"] for c in range(8)])
        stages = [("load_x",), ("fox_post", rnd), ("ffn", f"f2_{lf}", lf, V_F2 + lf * KC)]
        if rnd == 0:
            stages.append(("ffn", "f1_2", 2, V_F1 + 2 * KC))
        stages.append(("store_x",))
        r = run(stages, [{"xT_in": xts[c], "oT_in": o_ins[c]} for c in range(8)])
        xts = [r[c]["xT_out"] for c in range(8)]
    return from_fm(xts).astype(np.float32)
```

```python
import numpy as np
import ml_dtypes
import concourse.bass as bass
import concourse.mybir as mybir
from concourse.bass_utils import run_bass_kernel_spmd

F32 = mybir.dt.float32
BF16 = mybir.dt.bfloat16
AF = mybir.ActivationFunctionType
ALU = mybir.AluOpType

D = 2048
FF = 5632
T = 1024
SEQ = 4096
KC = D // 128
NFC = FF // 128
G = 4
NG = NFC // G
DEPTH = 4
EPS = 1e-6
NH = 4
DH = 128

V_F1 = 0
V_F2 = V_F1 + 4 * KC
V_MX = V_F2 + 4 * KC
V_PS = V_MX + 4 * KC
V_QG = V_PS + 2 * KC
V_KG = V_QG + 2
V_HF = V_KG + 2
NV = V_HF + 1


class Sem:
    def __init__(self, h):
        self.h = h
        self.n = 0


class Prog:
    def __init__(self):
        self.q = {e: [] for e in ("pe", "act", "dve", "pool", "sp")}
        self.waited = {e: {} for e in self.q}

    def op(self, eng, fn, deps=(), sem=None, by=1):
        ws = []
        for d in deps:
            if d is None:
                continue
            s, v = d
            if v <= 0:
                continue
            if self.waited[eng].get(id(s), 0) >= v:
                continue
            self.waited[eng][id(s)] = v
            ws.append((s.h, v))
        tok = None
        if sem is not None:
            sem.n += by
            tok = (sem, sem.n)

        def run(e, ws=ws, fn=fn, sem=sem, by=by):
            for h, v in ws:
                e.wait_ge(h, v)
            ins = fn(e)
            if sem is not None:
                ins.then_inc(sem.h, by)

        self.q[eng].append(run)
        return tok

    def wait_only(self, eng, deps):
        ws = []
        for d in deps:
            if d is None:
                continue
            s, v = d
            if self.waited[eng].get(id(s), 0) >= v:
                continue
            self.waited[eng][id(s)] = v
            ws.append((s.h, v))

        def run(e, ws=ws):
            for h, v in ws:
                e.wait_ge(h, v)

        self.q[eng].append(run)


class Builder:
    def __init__(self, stages, n_in_extra=None):
        self.stages = stages
        self.nc = bass.Bass("TRN2", target_bir_lowering=False)
        self.P = Prog()
        self.dram_in = {}
        self.dram_out = {}

    def din(self, name, shape, dt=F32):
        if name not in self.dram_in:
            self.dram_in[name] = self.nc.dram_tensor(name, list(shape), dt, kind="ExternalInput").ap()
        return self.dram_in[name]

    def dout(self, name, shape, dt=F32):
        if name not in self.dram_out:
            self.dram_out[name] = self.nc.dram_tensor(name, list(shape), dt, kind="ExternalOutput").ap()
        return self.dram_out[name]

    def build(self):
        nc = self.nc
        P = self.P
        from contextlib import ExitStack
        with ExitStack() as es:
            def sb(name, shape, dt):
                return es.enter_context(nc.sbuf_tensor("sb_" + name, list(shape), dt))

            def sem(name):
                return Sem(es.enter_context(nc.semaphore(name)))

            self.fused = any(st[0] == "fused" for st in self.stages)
            self.stages = [st for st in self.stages if st[0] != "fused"]
            self.xT = sb("xT", [128, KC, T], F32)
            self.hn = sb("hn", [128, KC, T], BF16)
            self.sgh = sb("sgh", [128, 2 * G, T], BF16)
            self.sg = self.sgh[:, 0:G, :]
            self.hT = self.sgh[:, G:2 * G, :]
            self.rstd = sb("rstd", [128, T], F32)
            self.ring = [sb(f"ring{i}", [128, 8192], BF16) for i in range(4)]
            self.hn_flat = self.hn[:].rearrange("p a b -> p (a b)")
            self.hn_f32 = self.hn_flat.bitcast(F32)
            self.sgh_flat = self.sgh[:].rearrange("p a b -> p (a b)")
            self.sgh_f32 = self.sgh_flat.bitcast(F32)
            self.ones = sb("ones_sb", [128, 128], BF16)
            self.vecs = sb("vecs_sb", [128, NV], F32)
            self.ps = [es.enter_context(nc.psum_tensor(f"ps{i}", [128, 512], F32)) for i in range(8)]
            self.es_sb = sb

            self.s_pe = sem("s_pe")
            self.s_act = sem("s_act")
            self.s_dve = sem("s_dve")
            self.s_pool = sem("s_pool")
            self.s_ring = [sem(f"s_ring{i}") for i in range(4)]
            self.s_ld = sem("s_ld")
            self.s_st = sem("s_st")
            self.s_ob = [sem("s_ob0"), sem("s_ob1")]
            self.s_cc = [sem("s_cc0"), sem("s_cc1")]
            self.s_cw = [sem("s_cw0"), sem("s_cw1")]
            self.s_cr = [sem("s_cr0"), sem("s_cr1")]
            self.all_sems = [self.s_pe, self.s_act, self.s_dve, self.s_pool, self.s_ld, self.s_st] + self.s_ring + self.s_ob + self.s_cc + self.s_cw + self.s_cr
            self.src_free = [None, None]
            self.dst_free = [None, None]
            if self.fused:
                self.cc_src = [nc.dram_tensor(f"cc_src{i}", [128, 2048], BF16) for i in range(2)]
                self.cc_dst = [nc.dram_tensor(f"cc_dst{i}", [512, 2048], BF16) for i in range(2)]
                self.hn_all_d = nc.dram_tensor("hn_all_i", [128, KC, SEQ], BF16).ap()
                self.oT_all_d = nc.dram_tensor("oT_all_i", [128, KC, SEQ], BF16).ap()

            self.ring_idx = 0
            self.ring_free = [None] * 4
            self.bank_free = [None] * 8
            self.gu_rr = 0
            self.dn_rr = 0
            self.x_ready = None
            self.hn_free = None
            self.sg_free = [None] * G
            self.hT_free = [None] * G
            self.rstd_free = None
            self.pre_sq = None

            t_ones = P.op("pool", lambda e: e.memset(self.ones[:], 1.0), sem=self.s_pool)
            self.ones_ready = t_ones
            vecs_d = self.din("vecs", [128, NV])
            self.vecs_ready = P.op("sp", lambda e: e.dma_start(out=self.vecs[:], in_=vecs_d), sem=self.s_ld, by=16)

            for st in self.stages:
                getattr(self, "st_" + st[0])(*st[1:])

            P.wait_only("sp", [(x, x.n) for x in self.all_sems])

            blk = es.enter_context(nc.Block())
            q = P.q

            @blk.tensor
            def _(e):
                for f in q["pe"]:
                    f(e)

            @blk.scalar
            def _(e):
                for f in q["act"]:
                    f(e)

            @blk.vector
            def _(e):
                for f in q["dve"]:
                    f(e)

            @blk.gpsimd
            def _(e):
                for f in q["pool"]:
                    f(e)

            @blk.sync
            def _(e):
                if self.fused:
                    self.pid_sp = nc.partition_id(engines=[mybir.EngineType.SP])
                for f in q["sp"]:
                    f(e)
        return nc


    def barrier(self):
        deps = [(x, x.n) for x in self.all_sems]
        for eng in ("pe", "act", "dve", "pool", "sp"):
            self.P.wait_only(eng, deps)

    GROUPS = [[0, 1, 2, 3], [4, 5, 6, 7]]

    def cc_write(self, i, fn, deps, eng="sp"):
        return self.P.op(eng, lambda e: fn(e, self.cc_src[i]), deps=list(deps) + [self.src_free[i]], sem=self.s_cw[i], by=16)

    def cc_go(self, i):
        P = self.P
        tok = P.op("pool", lambda e: e.collective_compute("AllGather", ALU.bypass, replica_groups=self.GROUPS,
                                                          ins=[self.cc_src[i].ap()], outs=[self.cc_dst[i].ap()]),
                   deps=[(self.s_cw[i], self.s_cw[i].n), self.dst_free[i]], sem=self.s_cc[i], by=1)
        self.src_free[i] = tok
        return tok

    def cc_read(self, i, fn, ctok, eng="pool"):
        tok = self.P.op(eng, lambda e: fn(e, self.cc_dst[i]), deps=[ctok], sem=self.s_cr[i], by=16)
        self.dst_free[i] = tok
        return tok
    def load_w(self, src_ap, view, eng="pool"):
        P = self.P
        r = self.ring_idx
        self.ring_idx += 1
        slot = r % 4
        dst = view(self.ring[slot])
        tok = P.op(eng, lambda e: e.dma_start(out=dst, in_=src_ap),
                   deps=[self.ring_free[slot]], sem=self.s_ring[slot], by=16)
        return slot, tok

    def gu_banks(self):
        i = self.gu_rr % 2
        self.gu_rr += 1
        return (0, 1) if i == 0 else (2, 3)

    def dn_banks(self):
        i = self.dn_rr % 2
        self.dn_rr += 1
        return (4, 5) if i == 0 else (6, 7)

    def st_load_x(self):
        P = self.P
        xin = self.din("xT_in", [128, KC, T])
        for i in range(4):
            P.op("sp", lambda e, i=i: e.dma_start(out=self.xT[:, 4 * i:4 * i + 4, :], in_=xin[:, 4 * i:4 * i + 4, :]),
                 sem=self.s_ld, by=16)
        self.x_ready = (self.s_ld, self.s_ld.n)

    def st_store_x(self):
        P = self.P
        xout = self.dout("xT_out", [128, KC, T])
        for i in range(4):
            P.op("sp", lambda e, i=i: e.dma_start(out=xout[:, 4 * i:4 * i + 4, :], in_=self.xT[:, 4 * i:4 * i + 4, :]),
                 deps=[self.x_ready], sem=self.s_st, by=16)

    def rmsnorm(self, vcol):
        P = self.P
        xT, hn, rstd = self.xT, self.hn, self.rstd
        if self.pre_sq:
            sq_tok = self.pre_sq
        else:
            sq_tok = []
            for kc in range(KC):
                sq_tok.append(P.op("act", lambda e, kc=kc: e.activation(out=hn[:, kc, :], in_=xT[:, kc, :], func=AF.Square),
                                   deps=[self.x_ready, self.hn_free], sem=self.s_act))
        self.pre_sq = None
        banks = self.gu_banks()
        for th in range(2):
            for kc in range(KC):
                last = kc == KC - 1
                tok = P.op("pe", lambda e, th=th, kc=kc: e.matmul(self.ps[banks[th]][:], self.ones[:], hn[:, kc, th * 512:(th + 1) * 512],
                                                                   start=(kc == 0), stop=(kc == KC - 1)),
                           deps=[sq_tok[kc], self.ones_ready, self.bank_free[banks[th]]],
                           sem=self.s_pe if last else None)
        ssq_tok = tok
        for th in range(2):
            sl = slice(th * 512, (th + 1) * 512)
            t1 = P.op("dve", lambda e, th=th, sl=sl: e.tensor_scalar(out=rstd[:, sl], in0=self.ps[banks[th]][:], scalar1=1.0 / D, scalar2=EPS,
                                                                      op0=ALU.mult, op1=ALU.add),
                      deps=[ssq_tok, self.rstd_free], sem=self.s_dve)
            self.bank_free[banks[th]] = t1
            t15 = P.op("act", lambda e, sl=sl: e.activation(out=rstd[:, sl], in_=rstd[:, sl], func=AF.Sqrt),
                       deps=[t1], sem=self.s_act)
            t2 = P.op("dve", lambda e, sl=sl: e.reciprocal(out=rstd[:, sl], in_=rstd[:, sl]),
                      deps=[t15], sem=self.s_dve)
        toks = []
        for kc in range(KC):
            tok = P.op("dve", lambda e, kc=kc: e.scalar_tensor_tensor(out=hn[:, kc, :], in0=xT[:, kc, :],
                                                                       scalar=self.vecs[:, vcol + kc:vcol + kc + 1], in1=rstd[:],
                                                                       op0=ALU.mult, op1=ALU.mult),
                       deps=[t2, ssq_tok, self.vecs_ready], sem=self.s_dve)
            toks.append(tok)
        self.hn_ready = tok
        self.rstd_free = tok
        return toks

    def st_ffn(self, wname, layer, vcol):
        P = self.P
        wg = self.din(wname + "g", [D, FF]).rearrange("(kc p) f -> p kc f", p=128)
        wu = self.din(wname + "u", [D, FF]).rearrange("(kc p) f -> p kc f", p=128)
        wd = self.din(wname + "d", [FF, D]).rearrange("(fc p) d -> p fc d", p=128)
        v_gu = lambda r: r[:].rearrange("p (a b) -> p a b", a=KC)
        v_d = lambda r: r[:].rearrange("p (a b) -> p a b", a=G)

        phases = [("gate", 0), ("up", 0)]
        for g in range(1, NG):
            phases += [("gate", g), ("down", g - 1), ("up", g)]
        phases.append(("down", NG - 1))

        loads = {}

        def issue(ph):
            kind, g = ph
            if kind == "gate":
                loads[ph] = self.load_w(wg[:, :, g * 512:(g + 1) * 512], v_gu)
            elif kind == "up":
                loads[ph] = self.load_w(wu[:, :, g * 512:(g + 1) * 512], v_gu)
            else:
                loads[ph] = self.load_w(wd[:, g * G:(g + 1) * G, :], v_d)

        hn_tok = self.rmsnorm(vcol)
        hn, sg, hT, xT = self.hn, self.sg, self.hT, self.xT

        silu_done = {}
        h_done = {}
        last_pe_hn = None
        pre_sq = []
        nissued = 0
        for pi, ph in enumerate(phases):
            while nissued < len(phases) and nissued < pi + 3:
                issue(phases[nissued])
                nissued += 1
            kind, g = ph
            slot, wtok = loads[ph]
            if kind in ("gate", "up"):
                wv = v_gu(self.ring[slot])
                for fc in range(G):
                    banks = self.gu_banks()
                    for kc in range(KC):
                        for th in range(2):
                            last = (kc == KC - 1 and th == 1)
                            tok = P.op("pe", lambda e, kc=kc, th=th, fc=fc, wv=wv, banks=banks: e.matmul(
                                self.ps[banks[th]][:], wv[:, kc, fc * 128:(fc + 1) * 128], hn[:, kc, th * 512:(th + 1) * 512],
                                start=(kc == 0), stop=(kc == KC - 1)),
                                deps=[wtok, hn_tok[kc], self.bank_free[banks[0]], self.bank_free[banks[1]]],
                                sem=self.s_pe if last else None)
                    mm_tok = tok
                    last_pe_hn = tok
                    if kind == "gate":
                        for th in range(2):
                            sl = slice(th * 512, (th + 1) * 512)
                            t = P.op("act", lambda e, fc=fc, th=th, sl=sl, banks=banks: e.activation(
                                out=sg[:, fc, sl], in_=self.ps[banks[th]][:], func=AF.Silu),
                                deps=[mm_tok, self.sg_free[fc]], sem=self.s_act)
                            self.bank_free[banks[th]] = t
                        silu_done[(g, fc)] = t
                    else:
                        for th in range(2):
                            sl = slice(th * 512, (th + 1) * 512)
                            t = P.op("dve", lambda e, fc=fc, th=th, sl=sl, banks=banks: e.tensor_tensor(
                                out=hT[:, fc, sl], in0=sg[:, fc, sl], in1=self.ps[banks[th]][:], op=ALU.mult),
                                deps=[mm_tok, silu_done[(g, fc)], self.hT_free[fc]], sem=self.s_dve)
                            self.bank_free[banks[th]] = t
                        h_done[(g, fc)] = t
                        self.sg_free[fc] = t
                self.ring_free[slot] = mm_tok
            else:
                wv = v_d(self.ring[slot])
                for oc in range(KC):
                    banks = self.dn_banks()
                    for fc in range(G):
                        for th in range(2):
                            last = (fc == G - 1 and th == 1)
                            tok = P.op("pe", lambda e, oc=oc, th=th, fc=fc, wv=wv, banks=banks: e.matmul(
                                self.ps[banks[th]][:], wv[:, fc, oc * 128:(oc + 1) * 128], hT[:, fc, th * 512:(th + 1) * 512],
                                start=(fc == 0), stop=(fc == G - 1)),
                                deps=[wtok, h_done[(g, fc)], self.bank_free[banks[0]], self.bank_free[banks[1]]],
                                sem=self.s_pe if last else None)
                    mm_tok = tok
                    for th in range(2):
                        sl = slice(th * 512, (th + 1) * 512)
                        t = P.op("dve", lambda e, oc=oc, th=th, sl=sl, banks=banks: e.scalar_tensor_tensor(
                            out=xT[:, oc, sl], in0=self.ps[banks[th]][:], scalar=0.5, in1=xT[:, oc, sl],
                            op0=ALU.mult, op1=ALU.add),
                            deps=[mm_tok], sem=self.s_dve)
                        self.bank_free[banks[th]] = t
                    if g == NG - 1:
                        pre_sq.append(P.op("act", lambda e, oc=oc: e.activation(out=hn[:, oc, :], in_=xT[:, oc, :], func=AF.Square),
                                           deps=[t, last_pe_hn], sem=self.s_act))
                self.ring_free[slot] = mm_tok
                for fc in range(G):
                    self.hT_free[fc] = mm_tok
                self.x_ready = t
        self.hn_free = last_pe_hn
        self.pre_sq = pre_sq


    def rms_stats(self, src, n, out_rstd, scale, bias, sq_dst):
        P = self.P
        sq_tok = []
        for kc in range(KC):
            sq_tok.append(P.op("act", lambda e, kc=kc: e.activation(out=sq_dst[:, kc, 0:n], in_=src[:, kc, 0:n], func=AF.Square),
                               deps=[self.x_ready, self.hn_free, (self.s_ld, self.s_ld.n)], sem=self.s_act))
        bank = self.gu_banks()[0]
        for kc in range(KC):
            tok = P.op("pe", lambda e, kc=kc: e.matmul(self.ps[bank][:, 0:n], self.ones[:], sq_dst[:, kc, 0:n],
                                                       start=(kc == 0), stop=(kc == KC - 1)),
                       deps=[sq_tok[kc], self.ones_ready, self.bank_free[bank]],
                       sem=self.s_pe if kc == KC - 1 else None)
        t1 = P.op("dve", lambda e: e.tensor_scalar(out=out_rstd[:, 0:n], in0=self.ps[bank][:, 0:n], scalar1=scale, scalar2=bias,
                                                   op0=ALU.mult, op1=ALU.add), deps=[tok], sem=self.s_dve)
        self.bank_free[bank] = t1
        t15 = P.op("act", lambda e: e.activation(out=out_rstd[:, 0:n], in_=out_rstd[:, 0:n], func=AF.Sqrt), deps=[t1], sem=self.s_act)
        t2 = P.op("dve", lambda e: e.reciprocal(out=out_rstd[:, 0:n], in_=out_rstd[:, 0:n]), deps=[t15], sem=self.s_dve)
        self.hn_free = tok
        return t2


    def halo_exchange(self):
        self.halo_start()
        self.halo_finish()

    def halo_start(self):
        self.cc_write(0, lambda e, src: e.dma_start(
            out=src.ap().bitcast(F32)[:, 0:256].rearrange("p (a b) -> p a b", a=KC), in_=self.xT[:, :, T - 16:T]), [self.x_ready])
        ctok = self.cc_go(0)

        def rd(e, dst):
            prev = (self.pid_sp + 3) % 4
            return e.dma_start(out=self.halo[:], in_=dst.ap().bitcast(F32).rearrange("(r p) c -> r p c", r=4)[bass.ds(prev, 1), :, 0:256]
                               .rearrange("r p (a b) -> p (r a) b", a=KC))
        self.cc_read(0, rd, ctok, eng="sp")

    def halo_finish(self):
        P = self.P
        P.wait_only("dve", [(self.s_cr[0], self.s_cr[0].n)])
        self.halo_tok = P.op("dve", lambda e: e.tensor_scalar(out=self.halo[:], in0=self.halo[:], scalar1=self.vecs[:, V_HF:V_HF + 1], scalar2=None, op0=ALU.mult),
                             deps=[self.vecs_ready], sem=self.s_dve)
        P.wait_only("act", [self.halo_tok])

    def st_halo_dbg(self):
        sb = self.es_sb
        self.barrier()
        if not hasattr(self, "halo"):
            self.halo = sb("halo", [128, KC, 16], F32)
        self.halo_exchange()
        P = self.P
        out = self.dout("halo_dbg", [128, KC, 16])
        self.P.op("sp", lambda e: e.dma_start(out=out, in_=self.halo[:]), deps=[(self.s_dve, self.s_dve.n)], sem=self.s_st, by=16)

    def st_pool(self, j, layer):
        P = self.P
        sb = self.es_sb
        xT, hn = self.xT, self.hn
        L = T + 16
        self.barrier()
        if not hasattr(self, "halo"):
            self.halo = sb("halo", [128, KC, 16], F32)
            self.halo_sq = sb("halo_sq", [128, KC, 16], BF16)
            self.rstd_h = sb("rstd_h", [128, 16], F32)
            self.invc = sb("invc", [128, 4, 16], F32)
        self.hp = self.sgh_f32[:, 0:L]
        self.wa = self.sgh_f32[:, L:2 * L]
        self.wb = self.sgh_f32[:, 2 * L:3 * L]
        invc_d = self.din("invc", [128, 4, 16])
        if not self.fused:
            halo_d = self.din(f"halo{layer}", [128, KC, 16])
            P.op("sp", lambda e: e.dma_start(out=self.halo[:], in_=halo_d), sem=self.s_ld, by=16)
        else:
            self.halo_start()
        P.op("sp", lambda e: e.dma_start(out=self.invc[:], in_=invc_d), sem=self.s_ld, by=16)
        ld_tok = (self.s_ld, self.s_ld.n)
        pw = self.din(f"poolw{j}", [4, 512, 512]).rearrange("g (c p) d -> p g c d", p=128)
        slot, wtok = self.load_w(pw, lambda r: r[:].rearrange("p (g c d) -> p g c d", g=4, c=4))
        wv = self.ring[slot][:].rearrange("p (g c d) -> p g c d", g=4, c=4)
        vcol = V_MX + layer * KC
        r_main = self.rmsnorm_stats_main()
        if self.fused:
            self.halo_finish()
        r_halo = self.rms_stats(self.halo, 16, self.rstd_h, 1.0 / D, EPS, self.halo_sq)
        hp, wa, wb = self.hp, self.wa, self.wb
        prev_read = None
        for kc in range(KC):
            gi = kc // 4
            w = 2 ** (gi + 1)
            gcol = self.vecs[:, vcol + kc:vcol + kc + 1]
            a = P.op("dve", lambda e, kc=kc, gcol=gcol: e.scalar_tensor_tensor(out=hp[:, 16:L], in0=xT[:, kc, :], scalar=gcol, in1=self.rstd[:],
                                                                              op0=ALU.mult, op1=ALU.mult),
                     deps=[r_main, self.vecs_ready, ld_tok, prev_read], sem=self.s_dve)
            b = P.op("dve", lambda e, kc=kc, gcol=gcol: e.scalar_tensor_tensor(out=hp[:, 0:16], in0=self.halo[:, kc, :], scalar=gcol, in1=self.rstd_h[:],
                                                                              op0=ALU.mult, op1=ALU.mult),
                     deps=[r_halo, a], sem=self.s_dve)
            src = hp
            last = b
            for sidx in range(gi + 1):
                sh = 2 ** sidx
                dst = wa if sidx % 2 == 0 else wb
                last = P.op("dve", lambda e, src=src, dst=dst, sh=sh: e.tensor_tensor(out=dst[:, sh:L], in0=src[:, sh:L], in1=src[:, 0:L - sh], op=ALU.add),
                            deps=[last], sem=self.s_dve)
                src = dst
            c1 = P.op("dve", lambda e, kc=kc, src=src, w=w: e.scalar_tensor_tensor(out=hn[:, kc, :], in0=src[:, 16:L], scalar=1.0 / w, in1=hp[:, 16:L],
                                                                                   op0=ALU.mult, op1=ALU.subtract),
                      deps=[last, self.hn_free], sem=self.s_dve)
            tmp = wb if src is wa else wa
            c2 = P.op("dve", lambda e, src=src, gi=gi, tmp=tmp: e.tensor_tensor(out=tmp[:, 0:16], in0=src[:, 16:32], in1=self.invc[:, gi, :], op=ALU.mult),
                      deps=[c1], sem=self.s_dve)
            c3 = P.op("dve", lambda e, kc=kc, tmp=tmp: e.tensor_tensor(out=hn[:, kc, 0:16], in0=tmp[:, 0:16], in1=hp[:, 16:32], op=ALU.subtract),
                      deps=[c2], sem=self.s_dve)
            prev_read = c3
        diff_tok = prev_read
        pscol = V_PS + j * KC
        for g in range(4):
            for oc in range(4):
                banks = self.dn_banks()
                for c in range(4):
                    for th in range(2):
                        last = (c == 3 and th == 1)
                        tok = P.op("pe", lambda e, g=g, oc=oc, c=c, th=th, banks=banks: e.matmul(
                            self.ps[banks[th]][:], wv[:, g, c, oc * 128:(oc + 1) * 128], hn[:, g * 4 + c, th * 512:(th + 1) * 512],
                            start=(c == 0), stop=(c == 3)),
                            deps=[wtok, diff_tok, self.bank_free[banks[0]], self.bank_free[banks[1]]],
                            sem=self.s_pe if last else None)
                mm_tok = tok
                o = g * 4 + oc
                for th in range(2):
                    sl = slice(th * 512, (th + 1) * 512)
                    t = P.op("dve", lambda e, o=o, th=th, sl=sl, banks=banks: e.scalar_tensor_tensor(
                        out=xT[:, o, sl], in0=self.ps[banks[th]][:], scalar=self.vecs[:, pscol + o:pscol + o + 1], in1=xT[:, o, sl],
                        op0=ALU.mult, op1=ALU.add), deps=[mm_tok], sem=self.s_dve)
                    self.bank_free[banks[th]] = t
        self.ring_free[slot] = mm_tok
        self.hn_free = mm_tok
        self.x_ready = t
        self.pre_sq = None

    def rmsnorm_stats_main(self):
        P = self.P
        xT, hn, rstd = self.xT, self.hn, self.rstd
        if self.pre_sq:
            sq_tok = self.pre_sq
        else:
            sq_tok = []
            for kc in range(KC):
                sq_tok.append(P.op("act", lambda e, kc=kc: e.activation(out=hn[:, kc, :], in_=xT[:, kc, :], func=AF.Square),
                                   deps=[self.x_ready, self.hn_free], sem=self.s_act))
        self.pre_sq = None
        banks = self.gu_banks()
        for th in range(2):
            for kc in range(KC):
                last = kc == KC - 1
                tok = P.op("pe", lambda e, th=th, kc=kc: e.matmul(self.ps[banks[th]][:], self.ones[:], hn[:, kc, th * 512:(th + 1) * 512],
                                                                   start=(kc == 0), stop=(kc == KC - 1)),
                           deps=[sq_tok[kc], self.ones_ready, self.bank_free[banks[th]]],
                           sem=self.s_pe if last else None)
        ssq_tok = tok
        for th in range(2):
            sl = slice(th * 512, (th + 1) * 512)
            t1 = P.op("dve", lambda e, th=th, sl=sl: e.tensor_scalar(out=rstd[:, sl], in0=self.ps[banks[th]][:], scalar1=1.0 / D, scalar2=EPS,
                                                                      op0=ALU.mult, op1=ALU.add),
                      deps=[ssq_tok, self.rstd_free], sem=self.s_dve)
            self.bank_free[banks[th]] = t1
            t15 = P.op("act", lambda e, sl=sl: e.activation(out=rstd[:, sl], in_=rstd[:, sl], func=AF.Sqrt), deps=[t1], sem=self.s_act)
            t2 = P.op("dve", lambda e, sl=sl: e.reciprocal(out=rstd[:, sl], in_=rstd[:, sl]), deps=[t15], sem=self.s_dve)
        self.hn_free = ssq_tok
        return t2

    def st_fox_pre(self, layer):
        P = self.P
        toks = self.rmsnorm(V_MX + layer * KC)
        tok = toks[-1]
        if not self.fused:
            hn_out = self.dout("hn_out", [128, KC, T], BF16)
            for i in range(2):
                P.op("sp", lambda e, i=i: e.dma_start(out=hn_out[:, 8 * i:8 * i + 8, :], in_=self.hn[:, 8 * i:8 * i + 8, :]),
                     deps=[tok], sem=self.s_st, by=16)
            self.hn_free = (self.s_st, self.s_st.n)
            return
        hv = self.hn_all_d.rearrange("p kc (r t) -> r p kc t", r=4)

        def rd_all(pk, pi, pc):
            for r in range(4):
                self.cc_read(pi, lambda e, dst, pk=pk, r=r: e.dma_start(
                    out=hv[r, :, 2 * pk:2 * pk + 2, :],
                    in_=dst.ap()[r * 128:(r + 1) * 128, :].rearrange("p (a b) -> p a b", a=2)), pc)

        prev = None
        for k in range(8):
            i = k % 2
            self.cc_write(i, lambda e, src, k=k: e.dma_start(out=src.ap().rearrange("p (a b) -> p a b", a=2), in_=self.hn[:, 2 * k:2 * k + 2, :]), [toks[2 * k + 1]])
            ctok = self.cc_go(i)
            if prev is not None:
                rd_all(*prev)
            prev = (k, i, ctok)
        rd_all(*prev)
        self.hn_free = (self.s_cw[1], self.s_cw[1].n)

    def st_fox_post(self, jf):
        P = self.P
        hn, xT = self.hn, self.xT
        self.barrier()
        if not self.fused:
            o_in = self.din("oT_in", [128, KC, T], BF16)
            for i in range(2):
                P.op("sp", lambda e, i=i: e.dma_start(out=hn[:, 8 * i:8 * i + 8, :], in_=o_in[:, 8 * i:8 * i + 8, :]),
                     deps=[self.hn_free], sem=self.s_ld, by=16)
        else:
            for i in range(2):
                P.op("sp", lambda e, i=i: e.dma_start(out=hn[:, 8 * i:8 * i + 8, :],
                                                      in_=self.oT_all_d[:, 8 * i:8 * i + 8, bass.ds((self.pid_sp % 4) * T, T)]),
                     deps=[self.hn_free], sem=self.s_ld, by=16)
        o_tok = (self.s_ld, self.s_ld.n)
        wo = self.din(f"foxwo{jf}", [D, D]).rearrange("(h p) o -> p h o", p=128)
        v_gu = lambda r: r[:].rearrange("p (a b) -> p a b", a=KC)
        loads = [self.load_w(wo[:, :, r * 512:(r + 1) * 512], v_gu) for r in range(4)]
        for r in range(4):
            slot, wtok = loads[r]
            wv = v_gu(self.ring[slot])
            for oc4 in range(4):
                oc = r * 4 + oc4
                banks = self.dn_banks()
                for h in range(KC):
                    for th in range(2):
                        last = (h == KC - 1 and th == 1)
                        tok = P.op("pe", lambda e, h=h, th=th, oc4=oc4, wv=wv, banks=banks: e.matmul(
                            self.ps[banks[th]][:], wv[:, h, oc4 * 128:(oc4 + 1) * 128], hn[:, h, th * 512:(th + 1) * 512],
                            start=(h == 0), stop=(h == KC - 1)),
                            deps=[wtok, o_tok, self.bank_free[banks[0]], self.bank_free[banks[1]]],
                            sem=self.s_pe if last else None)
                mm_tok = tok
                for th in range(2):
                    sl = slice(th * 512, (th + 1) * 512)
                    t = P.op("dve", lambda e, oc=oc, th=th, sl=sl, banks=banks: e.tensor_tensor(
                        out=xT[:, oc, sl], in0=self.ps[banks[th]][:], in1=xT[:, oc, sl], op=ALU.add),
                        deps=[mm_tok, self.x_ready], sem=self.s_dve)
                    self.bank_free[banks[th]] = t
            self.ring_free[slot] = mm_tok
        self.hn_free = mm_tok
        self.x_ready = t

    def st_fox_attn(self, jf):
        P = self.P
        sb = self.es_sb
        NT = SEQ // 512
        NKT = SEQ // 128
        self.barrier()
        hn_all = self.hn_all_d if self.fused else self.din("hn_all", [128, KC, SEQ], BF16)
        wq = self.din(f"wq{jf}", [D, NH * DH]).rearrange("(kc p) f -> p kc f", p=128)
        wk = self.din(f"wk{jf}", [D, NH * DH]).rearrange("(kc p) f -> p kc f", p=128)
        wvd = self.din(f"wv{jf}", [D, NH * DH]).rearrange("(kc p) f -> p kc f", p=128)
        wf_d = self.din(f"wf96_{jf}", [D, 96]).rearrange("(kc p) f -> p kc f", p=128)
        b96_d = self.din(f"b96_{jf}", [96, 1])
        sel_d = self.din("sel", [96, NH, 128], BF16)
        id96_d = self.din("id96", [96, NH])
        mask_d = self.din("maskneg", [128, 4, 512], BF16)
        idb_d = self.din("identb", [128, 128], BF16)
        if not self.fused:
            oT_out = self.dout("oT_out", [128, NH, SEQ], BF16)
        if not hasattr(self, "qT_t"):
            self.qT_t = sb("qT", [128, SEQ], BF16)
            self.b96_t = sb("b96_sb", [96, 1], F32)
            self.sel_t = sb("sel_sb", [96, NH, 128], BF16)
            self.id96_t = sb("id96_sb", [96, NH], F32)
            self.mask_t = sb("mask_sb", [128, 4, 512], BF16)
            self.identb_t = sb("identb_sb", [128, 128], BF16)
            self.A96_t = sb("A96", [96, SEQ], BF16)
            self.negF_t = sb("negF", [128, NKT * NH], F32)
        qT = self.qT_t
        b96, sel, id96, maskneg, identb, A96, negF = self.b96_t, self.sel_t, self.id96_t, self.mask_t, self.identb_t, self.A96_t, self.negF_t
        wf = qT[:, 0:KC * 96].rearrange("p (a b) -> p a b", a=KC)
        TB = qT[0:96, :]
        Z = self.hn_f32[0:96, 0:SEQ]
        T1 = self.hn_f32[0:96, SEQ:2 * SEQ]
        F96 = self.sgh_f32[0:96, :]
        kT = self.sgh_flat[:, 0:SEQ]
        vv = self.sgh_flat[:, SEQ:2 * SEQ].rearrange("p (a b) -> p a b", a=NKT)
        hf = self.hn_flat
        wqkv = hf[:, 0:6144].rearrange("p (w a b) -> p w a b", w=3, a=KC)
        sqb = [hf[:, 6144:6656], hf[:, 6656:7168]]
        pT = [hf[:, 7168 + 512 * i:7168 + 512 * (i + 1)] for i in range(4)]
        obuf = [hf[:, 9216 + 512 * i:9216 + 512 * (i + 1)] for i in range(4)]
        rden = hf[:, 11264:12288].bitcast(F32)
        rb = [self.rstd[:, 0:512], self.rstd[:, 512:1024]]

        v_gu = lambda r: r[:].rearrange("p (a b) -> p a b", a=KC)
        for dst, src in ((b96, b96_d), (sel, sel_d), (id96, id96_d), (maskneg, mask_d), (identb, idb_d)):
            P.op("sp", lambda e, dst=dst, src=src: e.dma_start(out=dst[:], in_=src), sem=self.s_ld, by=16)
        const_tok = (self.s_ld, self.s_ld.n)
        wf_tok = P.op("pool", lambda e: e.dma_start(out=wf[:], in_=wf_d), sem=self.s_ld, by=16)
        wf_tok = (self.s_ld, self.s_ld.n)

        ztok = None
        for tt in range(NT):
            slot, htok = self.load_w(hn_all[:, :, tt * 512:(tt + 1) * 512], v_gu, eng="sp")
            hv = v_gu(self.ring[slot])
            bank = tt % 2
            for kc in range(KC):
                tok = P.op("pe", lambda e, kc=kc, hv=hv, bank=bank: e.matmul(self.ps[bank][0:96, :], wf[:, kc, :], hv[:, kc, :],
                                                                            start=(kc == 0), stop=(kc == KC - 1)),
                           deps=[htok, wf_tok, self.bank_free[bank]], sem=self.s_pe if kc == KC - 1 else None)
            self.ring_free[slot] = tok
            ztok = P.op("dve", lambda e, tt=tt, bank=bank: e.tensor_scalar(out=Z[:, tt * 512:(tt + 1) * 512], in0=self.ps[bank][0:96, :],
                                                                          scalar1=b96[:, 0:1], scalar2=None, op0=ALU.add),
                        deps=[tok, const_tok], sem=self.s_dve)
            self.bank_free[bank] = ztok
        t = P.op("act", lambda e: e.activation(out=T1[:], in_=Z[:], func=AF.Abs), deps=[ztok], sem=self.s_act)
        t = P.op("act", lambda e: e.activation(out=T1[:], in_=T1[:], func=AF.Exp, scale=-1.0), deps=[t], sem=self.s_act)
        t = P.op("dve", lambda e: e.tensor_scalar(out=T1[:], in0=T1[:], scalar1=1.0, scalar2=None, op0=ALU.add), deps=[t], sem=self.s_dve)
        t = P.op("act", lambda e: e.activation(out=T1[:], in_=T1[:], func=AF.Ln), deps=[t], sem=self.s_act)
        t2 = P.op("dve", lambda e: e.tensor_single_scalar(out=Z[:], in_=Z[:], scalar=0.0, op=ALU.min), deps=[t], sem=self.s_dve)
        t = P.op("dve", lambda e: e.tensor_tensor(out=Z[:], in0=Z[:], in1=T1[:], op=ALU.subtract), deps=[t2, t], sem=self.s_dve)
        t = P.op("dve", lambda e: e.tensor_tensor_scan(out=F96[:], data0=self.ones[0:96, 0:1].to_broadcast([96, SEQ]), data1=Z[:],
                                                       initial=0.0, op0=ALU.mult, op1=ALU.add),
                 deps=[t, self.ones_ready], sem=self.s_dve)
        f_tok = t
        t = P.op("dve", lambda e: e.tensor_copy(out=A96[:], in_=F96[:]), deps=[t], sem=self.s_dve)
        t = P.op("dve", lambda e: e.tensor_tensor(out=T1[:], in0=F96[:], in1=A96[:], op=ALU.subtract), deps=[t], sem=self.s_dve)
        t = P.op("dve", lambda e: e.tensor_copy(out=TB[:], in_=T1[:]), deps=[t], sem=self.s_dve)
        t = P.op("dve", lambda e: e.tensor_tensor(out=T1[64:96, :], in0=T1[64:96, :], in1=TB[64:96, :], op=ALU.subtract), deps=[t], sem=self.s_dve)
        t = P.op("dve", lambda e: e.tensor_copy(out=A96[32:64, :], in_=TB[32:64, :]), deps=[t], sem=self.s_dve)
        t = P.op("dve", lambda e: e.tensor_copy(out=A96[64:96, :], in_=T1[64:96, :]), deps=[t], sem=self.s_dve)
        a_tok = t
        for kt in range(NKT):
            tok = P.op("pe", lambda e, kt=kt: e.matmul(self.ps[2][:, kt * NH:(kt + 1) * NH], F96[:, kt * 128:(kt + 1) * 128], id96[:],
                                                       start=True, stop=True),
                       deps=[f_tok, const_tok, self.bank_free[2]], sem=self.s_pe if kt == NKT - 1 else None)
        t = P.op("dve", lambda e: e.tensor_scalar(out=negF[:], in0=self.ps[2][:, 0:NKT * NH], scalar1=-1.0, scalar2=None, op0=ALU.mult),
                 deps=[tok], sem=self.s_dve)
        self.bank_free[2] = t
        negf_tok = t

        self.barrier()
        pending_rd = None
        next_w_tok = None
        qv_free = None
        pT_free = [None] * 4
        obuf_free = [None] * 4
        rden_free = None
        p_rr = 0
        s_rr = 0
        o_rr = 0
        qcol = V_QG + jf
        kcol = V_KG + jf
        for h in range(NH):
            if h == 0:
                for wi, wsrc in enumerate((wq, wk, wvd)):
                    P.op("pool", lambda e, wi=wi, wsrc=wsrc: e.dma_start(out=wqkv[:, wi, :, :], in_=wsrc[:, :, 0:DH]),
                         deps=[(self.s_pe, self.s_pe.n)], sem=self.s_ld, by=16)
                w_tok = (self.s_ld, self.s_ld.n)
            else:
                w_tok = next_w_tok
            qk_ready = None
            for tt in range(NT):
                slot, htok = self.load_w(hn_all[:, :, tt * 512:(tt + 1) * 512], v_gu, eng="sp")
                hv = v_gu(self.ring[slot])
                mm = {}
                qkb = (0, 1) if tt % 2 == 0 else (5, 6)
                for wi, bank in ((0, qkb[0]), (1, qkb[1])):
                    for kc in range(KC):
                        tok = P.op("pe", lambda e, kc=kc, hv=hv, wi=wi, bank=bank: e.matmul(self.ps[bank][:], wqkv[:, wi, kc, :], hv[:, kc, :],
                                                                                           start=(kc == 0), stop=(kc == KC - 1)),
                                   deps=[htok, w_tok, self.bank_free[bank]], sem=self.s_pe if kc == KC - 1 else None)
                    mm[wi] = tok
                for s4 in range(4):
                    for kc in range(KC):
                        tok = P.op("pe", lambda e, kc=kc, hv=hv, s4=s4: e.matmul(self.ps[2][:, s4 * DH:(s4 + 1) * DH], hv[:, kc, s4 * 128:(s4 + 1) * 128],
                                                                                 wqkv[:, 2, kc, :], start=(kc == 0), stop=(kc == KC - 1)),
                                   deps=[htok, w_tok, self.bank_free[2]], sem=self.s_pe if (kc == KC - 1 and s4 == 3) else None)
                mm[2] = tok
                self.ring_free[slot] = tok
                tv = P.op("act", lambda e, tt=tt: e.activation(out=vv[:, tt * 4:(tt + 1) * 4, :], in_=self.ps[2][:].rearrange("p (a b) -> p a b", a=4),
                                                              func=AF.Copy), deps=[mm[2], qv_free], sem=self.s_act)
                self.bank_free[2] = tv
                sq_t = {}
                for wi in (0, 1):
                    sq_t[wi] = P.op("act", lambda e, wi=wi, qkb=qkb: e.activation(out=sqb[wi][:], in_=self.ps[qkb[wi]][:], func=AF.Square),
                                    deps=[mm[wi], self.bank_free[3 + wi]], sem=self.s_act)
                ss_t = {}
                for wi in (0, 1):
                    ss_t[wi] = P.op("pe", lambda e, wi=wi: e.matmul(self.ps[3 + wi][:], self.ones[:], sqb[wi][:], start=True, stop=True),
                                    deps=[sq_t[wi], self.ones_ready, self.bank_free[3 + wi]], sem=self.s_pe)
                for wi, dst, gcol in ((0, qT, qcol), (1, kT, kcol)):
                    sc, bi = (1.0, DH * EPS) if wi == 0 else (1.0 / DH, EPS)
                    t1 = P.op("dve", lambda e, wi=wi, sc=sc, bi=bi: e.tensor_scalar(out=rb[wi][:], in0=self.ps[3 + wi][:], scalar1=sc, scalar2=bi,
                                                                                    op0=ALU.mult, op1=ALU.add), deps=[ss_t[wi]], sem=self.s_dve)
                    self.bank_free[3 + wi] = t1
                    t15 = P.op("act", lambda e, wi=wi: e.activation(out=rb[wi][:], in_=rb[wi][:], func=AF.Sqrt), deps=[t1], sem=self.s_act)
                    t2 = P.op("dve", lambda e, wi=wi: e.reciprocal(out=rb[wi][:], in_=rb[wi][:]), deps=[t15], sem=self.s_dve)
                    t3 = P.op("dve", lambda e, wi=wi, dst=dst, gcol=gcol, tt=tt, qkb=qkb: e.scalar_tensor_tensor(
                        out=dst[:, tt * 512:(tt + 1) * 512], in0=self.ps[qkb[wi]][:], scalar=self.vecs[:, gcol:gcol + 1], in1=rb[wi][:],
                        op0=ALU.mult, op1=ALU.mult), deps=[t2, self.vecs_ready, qv_free], sem=self.s_dve)
                    self.bank_free[qkb[wi]] = t3
                qk_ready = t3
            v_ready = tv
            if h + 1 < NH:
                for wi, wsrc in enumerate((wq, wk, wvd)):
                    P.op("pool", lambda e, wi=wi, wsrc=wsrc, h=h: e.dma_start(out=wqkv[:, wi, :, :], in_=wsrc[:, :, (h + 1) * DH:(h + 2) * DH]),
                         deps=[(self.s_pe, self.s_pe.n)], sem=self.s_ld, by=16)
                next_w_tok = (self.s_ld, self.s_ld.n)
            last_pv = None
            for qt in range(NT):
                nkt = (qt + 1) * 4
                ob, db = (0, 1) if o_rr % 2 == 0 else (2, 3)
                o_rr += 1
                pinfo = {}

                def emit_S(kt, qt=qt, h=h):
                    nonlocal s_rr, p_rr
                    sbk = 4 + (s_rr % 4)
                    s_rr += 1
                    pi = p_rr % 4
                    p_rr += 1
                    diag = kt >= qt * 4
                    P.op("pe", lambda e: e.matmul(self.ps[sbk][:], kT[:, kt * 128:(kt + 1) * 128], qT[:, qt * 512:(qt + 1) * 512],
                                                  start=True, stop=False),
                         deps=[qk_ready, self.bank_free[sbk]])
                    tok = P.op("pe", lambda e: e.matmul(self.ps[sbk][:], sel[:, h, :], A96[:, qt * 512:(qt + 1) * 512],
                                                        start=False, stop=(not diag)),
                               deps=[a_tok, const_tok], sem=None if diag else self.s_pe)
                    if diag:
                        o = kt - qt * 4
                        tok = P.op("pe", lambda e: e.matmul(self.ps[sbk][:], identb[:], maskneg[:, o, :], start=False, stop=True),
                                   deps=[const_tok], sem=self.s_pe)
                    col = kt * NH + h
                    pt = P.op("act", lambda e: e.activation(out=pT[pi][:], in_=self.ps[sbk][:], func=AF.Exp, bias=negF[:, col:col + 1]),
                              deps=[tok, negf_tok, pT_free[pi]], sem=self.s_act)
                    self.bank_free[sbk] = pt
                    pinfo[kt] = (pi, pt)

                def emit_PV(kt, qt=qt, nkt=nkt, ob=ob, db=db):
                    pi, pt = pinfo[kt]
                    P.op("pe", lambda e: e.matmul(self.ps[ob][:], vv[:, kt, :], pT[pi][:], start=(kt == 0), stop=(kt == nkt - 1)),
                         deps=[pt, v_ready, self.bank_free[ob]])
                    tok = P.op("pe", lambda e: e.matmul(self.ps[db][:], self.ones[:], pT[pi][:], start=(kt == 0), stop=(kt == nkt - 1)),
                               deps=[self.bank_free[db], self.ones_ready], sem=self.s_pe)
                    pT_free[pi] = tok
                    return tok

                emit_S(0)
                emit_S(1)
                emit_S(2)
                for kt in range(nkt):
                    last_pv = emit_PV(kt)
                    if kt + 3 < nkt:
                        emit_S(kt + 3)
                oi = (qt % 4) if self.fused else (qt % 2)
                t1 = P.op("dve", lambda e, db=db: e.reciprocal(out=rden[:], in_=self.ps[db][:]), deps=[last_pv, rden_free], sem=self.s_dve)
                t2 = P.op("dve", lambda e, ob=ob, oi=oi: e.tensor_tensor(out=obuf[oi][:], in0=self.ps[ob][:], in1=rden[:], op=ALU.mult),
                          deps=[t1, obuf_free[oi]], sem=self.s_dve)
                rden_free = t2
                self.bank_free[ob] = t2
                self.bank_free[db] = t2
                if not self.fused:
                    P.op("sp", lambda e, oi=oi, h=h, qt=qt: e.dma_start(out=oT_out[:, h, qt * 512:(qt + 1) * 512], in_=obuf[oi][:]),
                         deps=[t2], sem=self.s_ob[oi], by=16)
                    obuf_free[oi] = (self.s_ob[oi], self.s_ob[oi].n)
                else:
                    ci = (qt // 4) % 2
                    self.cc_write(ci, lambda e, src, oi=oi, qt=qt: e.dma_start(out=src.ap()[:, (qt % 4) * 512:(qt % 4 + 1) * 512], in_=obuf[oi][:]), [t2], eng="pool")
                    if qt % 4 == 3:
                        for k4 in range(4):
                            obuf_free[k4] = (self.s_cw[ci], self.s_cw[ci].n)
                        if pending_rd is not None:
                            self.cc_read(*pending_rd)
                        ctok = self.cc_go(ci)
                        hfi = qt // 4
                        ov = self.oT_all_d.rearrange("p (hg hh) t -> hg p hh t", hg=4)
                        pending_rd = (ci, lambda e, dst, h=h, hfi=hfi: e.dma_start(out=ov[:, :, h, hfi * 2048:(hfi + 1) * 2048],
                                                                                  in_=dst.ap().rearrange("(r p) t -> r p t", r=4)), ctok)
            qv_free = last_pv
        if pending_rd is not None:
            self.cc_read(*pending_rd)
        self.barrier()
        self.hn_free = None

def to_fm(x):
    xs = x.reshape(8, T, KC, 128)
    return [np.ascontiguousarray(xs[c].transpose(2, 1, 0)) for c in range(8)]


def from_fm(xts):
    out = np.empty((8, T, KC, 128), np.float32)
    for c in range(8):
        out[c] = xts[c].transpose(2, 1, 0)
    return out.reshape(2, SEQ, D)


def make_vecs(inp, core=0):
    v = np.zeros((128, NV), np.float32)
    v[:, V_HF] = 0.0 if core % 4 == 0 else 1.0
    for l in range(DEPTH):
        v[:, V_F1 + l * KC:V_F1 + (l + 1) * KC] = inp["ffn1_norm"][l].reshape(KC, 128).T
        v[:, V_F2 + l * KC:V_F2 + (l + 1) * KC] = inp["ffn2_norm"][l].reshape(KC, 128).T
        v[:, V_MX + l * KC:V_MX + (l + 1) * KC] = inp["mix_norm"][l].reshape(KC, 128).T
    for j in range(2):
        v[:, V_PS + j * KC:V_PS + (j + 1) * KC] = inp["pool_scale"][j].reshape(KC, 128).T
        v[:, V_QG + j] = inp["fox_q_gain"][j]
        v[:, V_KG + j] = inp["fox_k_gain"][j]
    return v


_cache = {}
FUSED = True


def run_stage_list(stages, in_maps, trace=False):
    key = repr(stages)
    if key not in _cache:
        b = Builder(stages)
        nc = b.build()
        _cache[key] = (nc, b)
    nc, b = _cache[key]
    maps = [{k: m[k] for k in b.dram_in} for m in in_maps]
    res = run_bass_kernel_spmd(nc, maps, core_ids=list(range(8)), trace=trace)
    return res


BF = ml_dtypes.bfloat16
POOL_WINDOWS = (2, 4, 8, 16)


def _const_tables():
    sel = np.zeros((96, NH, 128), np.float32)
    id96 = np.zeros((96, NH), np.float32)
    for h in range(NH):
        for grp in range(3):
            sel[32 * grp + h, h, :] = 1.0
        id96[h, h] = 1.0
    p = np.arange(128)[:, None, None]
    o = np.arange(4)[None, :, None]
    c = np.arange(512)[None, None, :]
    maskneg = np.where(o * 128 + p <= c, 0.0, -30000.0).astype(np.float32)
    identb = np.eye(128, dtype=np.float32)
    return sel.astype(BF), id96, maskneg.astype(BF), identb.astype(BF)


def _invc(j):
    t = np.arange(16)
    out = np.zeros((128, 4, 16), np.float32)
    for g, w in enumerate(POOL_WINDOWS):
        cnt = np.minimum(t + 1, w) if j == 0 else np.full(16, w)
        out[:, g, :] = (1.0 / cnt.astype(np.float32))[None, :]
    return out


def kernel(**inp):
    inp = {k: np.asarray(v) for k, v in inp.items()}
    x = inp["x"].astype(np.float32, copy=False).reshape(8 * T, D)
    sel, id96, maskneg, identb = _const_tables()
    base = []
    for c in range(8):
        m = {"vecs": make_vecs(inp, c), "invc": _invc(c % 4), "sel": sel, "id96": id96, "maskneg": maskneg, "identb": identb}
        for l in range(DEPTH):
            for nm, pre in (("f1_", "ffn1_"), ("f2_", "ffn2_")):
                m[f"{nm}{l}g"] = inp[pre + "w_gate"][l]
                m[f"{nm}{l}u"] = inp[pre + "w_up"][l]
                m[f"{nm}{l}d"] = inp[pre + "w_down"][l]
        for j in range(2):
            m[f"poolw{j}"] = inp["pool_w"][j]
            m[f"foxwo{j}"] = inp["fox_w_out"][j]
        base.append(m)

    def fox_inputs(jf, maps):
        win = inp["fox_w_in"][jf]
        for c in range(8):
            hg = c % 4
            maps[c][f"wq{jf}"] = np.ascontiguousarray(win[:, hg * 512:(hg + 1) * 512])
            maps[c][f"wk{jf}"] = np.ascontiguousarray(win[:, D + hg * 512:D + (hg + 1) * 512])
            maps[c][f"wv{jf}"] = np.ascontiguousarray(win[:, 2 * D + hg * 512:2 * D + (hg + 1) * 512])
            wf96 = np.zeros((D, 96), np.float32)
            b96 = np.zeros((96, 1), np.float32)
            for h in range(NH):
                for grp in range(3):
                    wf96[:, 32 * grp + h] = win[:, 3 * D + hg * NH + h]
                    b96[32 * grp + h, 0] = inp["fox_b_f"][jf][hg * NH + h]
            maps[c][f"wf96_{jf}"] = wf96
            maps[c][f"b96_{jf}"] = b96

    def halos(xts):
        out = []
        for c in range(8):
            if c % 4 == 0:
                out.append(np.zeros((128, KC, 16), np.float32))
            else:
                out.append(np.ascontiguousarray(xts[c - 1][:, :, T - 16:T]))
        return out

    def run(stages, extra):
        maps = [dict(base[c], **extra[c]) for c in range(8)]
        return run_stage_list(stages, maps).results

    def attn_round(jf, hn_outs):
        extra = [dict() for _ in range(8)]
        for b in range(2):
            hn_all = np.ascontiguousarray(np.concatenate([hn_outs[b * 4 + j] for j in range(4)], axis=2))
            for j in range(4):
                extra[b * 4 + j]["hn_all"] = hn_all
        fox_inputs(jf, extra)
        r = run([("fox_attn", jf)], extra)
        o_ins = []
        for c in range(8):
            b, j = divmod(c, 4)
            o_ins.append(np.ascontiguousarray(np.concatenate(
                [r[b * 4 + hg]["oT_out"][:, :, j * T:(j + 1) * T] for hg in range(4)], axis=1)))
        return o_ins

    xts = to_fm(x)
    if FUSED:
        extra = [{"xT_in": xts[c]} for c in range(8)]
        fox_inputs(0, extra)
        fox_inputs(1, extra)
        stages = [("fused",), ("load_x",), ("ffn", "f1_0", 0, V_F1)]
        for rnd in range(2):
            lp, lf = 2 * rnd, 2 * rnd + 1
            stages += [("pool", rnd, lp), ("ffn", f"f2_{lp}", lp, V_F2 + lp * KC), ("ffn", f"f1_{lf}", lf, V_F1 + lf * KC),
                       ("fox_pre", lf), ("fox_attn", rnd), ("fox_post", rnd), ("ffn", f"f2_{lf}", lf, V_F2 + lf * KC)]
            if rnd == 0:
                stages.append(("ffn", "f1_2", 2, V_F1 + 2 * KC))
        stages.append(("store_x",))
        r = run(stages, extra)
        return from_fm([r[c]["xT_out"] for c in range(8)]).astype(np.float32)
    r = run([("load_x",), ("ffn", "f1_0", 0, V_F1 + 0 * KC), ("store_x",)], [{"xT_in": xts[c]} for c in range(8)])
    xts = [r[c]["xT_out"] for c in range(8)]
    for rnd in range(2):
        lp = 2 * rnd
        lf = 2 * rnd + 1
        hs = halos(xts)
        r = run([("load_x",), ("pool", rnd, lp), ("ffn", f"f2_{lp}", lp, V_F2 + lp * KC),
                 ("ffn", f"f1_{lf}", lf, V_F1 + lf * KC), ("fox_pre", lf), ("store_x",)],
                [{"xT_in": xts[c], f"halo{lp}": hs[c]} for c in range(8)])
        xts = [r[c]["xT_out"] for c in range(8)]
        o_ins = attn_round(rnd, [r[c]["hn_out"] for c in range(8)])
        stages = [("load_x",), ("fox_post", rnd), ("ffn", f"f2_{lf}", lf, V_F2 + lf * KC)]
        if rnd == 0:
            stages.append(("ffn", "f1_2", 2, V_F1 + 2 * KC))
        stages.append(("store_x",))
        r = run(stages, [{"xT_in": xts[c], "oT_in": o_ins[c]} for c in range(8)])
        xts = [r[c]["xT_out"] for c in range(8)]
    return from_fm(xts).astype(np.float32)
```
